# Optimizing a Trainium2 kernel written in Bass

```python
import math
import jax, jax.numpy as jnp
from jax import lax
import numpy as np


D_MODEL = 1024
BATCH = 32
SEQ = 2048
DEPTH = 4
DEC_BATCH = 32
DEC_SEQ = 64
PAST_LEN = 4096

CHUNK = 64
D_MIX = D_MODEL
A_HEADS = 4
A_HEAD_DIM = 64
A_WIDTH = A_HEADS * 2 * A_HEAD_DIM
ROT_DIM = A_HEAD_DIM // 4
ROPE_THETA = 500000.0
B_HEADS = 4
B_HEAD_DIM = 64
B_WIDTH = B_HEADS * B_HEAD_DIM
GMLP_CHUNK = 128
C_WIDTH = D_MIX - A_WIDTH - B_WIDTH
CONV_WIDTH = 31
D_IN = 3 * A_WIDTH + 2 * B_WIDTH + 2 * C_WIDTH
D_FF = 4 * D_MODEL
Q_BLOCK = 128
EPS = 1e-6
NEG_INF = -1e30

kernel_name = 'hymba_diffattn_gmlp_conformer_stream_step'


def rms_norm(x, g):
    xf = x.astype(jnp.float32)
    y = xf * lax.rsqrt(jnp.mean(xf * xf, axis=-1, keepdims=True) + EPS)
    return (y * g.astype(jnp.float32)).astype(x.dtype)


def layer_norm(x, g, b):
    xf = x.astype(jnp.float32)
    mu = jnp.mean(xf, axis=-1, keepdims=True)
    var = jnp.mean(jnp.square(xf - mu), axis=-1, keepdims=True)
    y = (xf - mu) * lax.rsqrt(var + EPS)
    return (y * g.astype(jnp.float32) + b.astype(jnp.float32)).astype(x.dtype)


def rope(x, pos):
    half = ROT_DIM // 2
    freqs = ROPE_THETA ** (-jnp.arange(0, ROT_DIM, 2, dtype=jnp.float32) / ROT_DIM)
    ang = pos[:, None] * freqs[None, :]
    cos = jnp.cos(ang)[None, :, None, :].astype(x.dtype)
    sin = jnp.sin(ang)[None, :, None, :].astype(x.dtype)
    x1 = x[..., :half]
    x2 = x[..., half:ROT_DIM]
    return jnp.concatenate([x1 * cos - x2 * sin, x2 * cos + x1 * sin, x[..., ROT_DIM:]], axis=-1)


def diff_lambda(lam_qk, layer_idx):
    lam_init = 0.8 - 0.6 * math.exp(-0.3 * layer_idx)
    lq = lam_qk.astype(jnp.float32)
    lam = jnp.exp(jnp.sum(lq[0] * lq[1])) - jnp.exp(jnp.sum(lq[2] * lq[3])) + lam_init
    return lam, lam_init


def diff_attend(q, k, v, lam, mask):
    bsz, nq = q.shape[0], q.shape[1]
    nk = k.shape[1]
    s = jnp.einsum('bqhd,bkhd->bhqk', q, k, preferred_element_type=jnp.float32) * (A_HEAD_DIM ** -0.5)
    if mask is not None:
        s = jnp.where(mask[None, None], s, NEG_INF)
    p = jax.nn.softmax(s, axis=-1).reshape(bsz, A_HEADS, 2, nq, nk)
    a = p[:, :, 0] - lam * p[:, :, 1]
    return jnp.einsum('bhqk,bkhe->bqhe', a.astype(v.dtype), v)


def diff_attn_prompt(q, k, v, lam):
    bsz, s_len = q.shape[0], q.shape[1]
    nb = s_len // Q_BLOCK
    qb = q.reshape(bsz, nb, Q_BLOCK, 2 * A_HEADS, A_HEAD_DIM).transpose(1, 0, 2, 3, 4)
    key_chunk = jnp.arange(s_len) // CHUNK

    def block(args):
        qi, i = args
        q_chunk = (i * Q_BLOCK + jnp.arange(Q_BLOCK)) // CHUNK
        mask = key_chunk[None, :] <= q_chunk[:, None]
        return diff_attend(qi, k, v, lam, mask)

    out = lax.map(block, (qb, jnp.arange(nb)))
    return out.transpose(1, 0, 2, 3, 4).reshape(bsz, s_len, A_HEADS, 2 * A_HEAD_DIM)


def gmlp_spatial_gate(zb, g_norm, w_s, bias):
    bsz, t = zb.shape[0], zb.shape[1]
    L = min(t, GMLP_CHUNK)
    nc = t // L
    u = zb[..., :B_WIDTH]
    vn = rms_norm(zb[..., B_WIDTH:], g_norm).reshape(bsz, t, B_HEADS, B_HEAD_DIM)
    tri = jnp.tril(jnp.ones((L, L), dtype=w_s.dtype))
    w = w_s[:, :L, :L] * tri[None]
    mixed = jnp.einsum('hts,bcshd->bcthd', w, vn.reshape(bsz, nc, L, B_HEADS, B_HEAD_DIM))
    mixed = mixed + bias[:, :L].T[None, None, :, :, None]
    return u * mixed.reshape(bsz, t, B_WIDTH), vn


def conv_module(conv_in, w, b, ln_g, ln_b):
    y = lax.conv_general_dilated(conv_in, w.reshape(CONV_WIDTH, 1, C_WIDTH), window_strides=(1,),
                                 padding='VALID', dimension_numbers=('NWC', 'WIO', 'NWC'),
                                 feature_group_count=C_WIDTH) + b
    y = layer_norm(y, ln_g, ln_b)
    return jax.nn.silu(y)


def encoder_layer(x, pos0, layer_idx, cache_k, cache_v, cache_conv, lw):
    (g_mix_pre, g_mix_post, g_mlp_pre, g_mlp_post, w_in, lam_qk, subln, gmlp_norm, gmlp_w_s,
     gmlp_bias, conv_w, conv_b, ln_g, ln_b, w_out, w_up, w_down) = lw
    bsz, t = x.shape[0], x.shape[1]
    h = rms_norm(x, g_mix_pre)
    z = jnp.einsum('btd,de->bte', h, w_in)
    q = z[..., :A_WIDTH].reshape(bsz, t, 2 * A_HEADS, A_HEAD_DIM)
    k = z[..., A_WIDTH:2 * A_WIDTH].reshape(bsz, t, 2 * A_HEADS, A_HEAD_DIM)
    v = z[..., 2 * A_WIDTH:3 * A_WIDTH].reshape(bsz, t, A_HEADS, 2 * A_HEAD_DIM)
    zb = jax.nn.gelu(z[..., 3 * A_WIDTH:3 * A_WIDTH + 2 * B_WIDTH])
    zc = z[..., 3 * A_WIDTH + 2 * B_WIDTH:]
    pos = (pos0 + jnp.arange(t)).astype(jnp.float32)
    q = rope(q, pos)
    k = rope(k, pos)
    lam, lam_init = diff_lambda(lam_qk, layer_idx)
    glu = zc[..., :C_WIDTH] * jax.nn.sigmoid(zc[..., C_WIDTH:])
    if cache_k is None:
        a = diff_attn_prompt(q, k, v, lam)
        conv_in = jnp.pad(glu, ((0, 0), (CONV_WIDTH - 1, 0), (0, 0)))
    else:
        a = diff_attend(q, jnp.concatenate([cache_k, k], axis=1),
                        jnp.concatenate([cache_v, v], axis=1), lam, None)
        conv_in = jnp.concatenate([cache_conv, glu], axis=1)
    new_conv = conv_in[:, conv_in.shape[1] - (CONV_WIDTH - 1):]
    a = (rms_norm(a, subln) * (1.0 - lam_init)).reshape(bsz, t, A_WIDTH)
    b_out, v_rows = gmlp_spatial_gate(zb, gmlp_norm, gmlp_w_s, gmlp_bias)
    c_out = conv_module(conv_in, conv_w, conv_b, ln_g, ln_b)
    mix = jnp.einsum('bte,ed->btd', jnp.concatenate([a, b_out, c_out], axis=-1), w_out)
    x = x + rms_norm(mix, g_mix_post)
    h = rms_norm(x, g_mlp_pre)
    f = jnp.einsum('btf,fd->btd', jnp.square(jax.nn.relu(jnp.einsum('btd,df->btf', h, w_up))), w_down)
    x = x + rms_norm(f, g_mlp_post)
    return x, k, v, new_conv, v_rows


def setup_inputs(seed: int = 0) -> dict:
    key = jax.random.key(seed)
    ks = jax.random.split(key, 24)

    def nrm(k, shape, scale):
        return scale * jax.random.normal(k, shape, jnp.float32)

    return {
        'x_prompt': nrm(ks[0], (BATCH, SEQ, D_MODEL), 1.0),
        'x_sample': nrm(ks[1], (DEC_BATCH, DEC_SEQ, D_MODEL), 1.0),
        'cache_k': nrm(ks[2], (DEPTH, DEC_BATCH, PAST_LEN, 2 * A_HEADS, A_HEAD_DIM), 1.0),
        'cache_v': nrm(ks[3], (DEPTH, DEC_BATCH, PAST_LEN, A_HEADS, 2 * A_HEAD_DIM), 1.0),
        'cache_conv': nrm(ks[4], (DEPTH, DEC_BATCH, CONV_WIDTH - 1, C_WIDTH), 0.5),
        'norm_mix_pre': 1.0 + nrm(ks[5], (DEPTH, D_MODEL), 0.05),
        'norm_mix_post': 1.0 + nrm(ks[6], (DEPTH, D_MODEL), 0.05),
        'norm_mlp_pre': 1.0 + nrm(ks[7], (DEPTH, D_MODEL), 0.05),
        'norm_mlp_post': 1.0 + nrm(ks[8], (DEPTH, D_MODEL), 0.05),
        'w_in': nrm(ks[9], (DEPTH, D_MODEL, D_IN), D_MODEL ** -0.5),
        'diff_lambda': nrm(ks[10], (DEPTH, 4, A_HEAD_DIM), 0.1),
        'diff_subln': 1.0 + nrm(ks[11], (DEPTH, 2 * A_HEAD_DIM), 0.05),
        'gmlp_norm': 1.0 + nrm(ks[12], (DEPTH, B_WIDTH), 0.05),
        'gmlp_w_s': nrm(ks[13], (DEPTH, B_HEADS, GMLP_CHUNK, GMLP_CHUNK), GMLP_CHUNK ** -0.5),
        'gmlp_bias': 1.0 + nrm(ks[14], (DEPTH, B_HEADS, GMLP_CHUNK), 0.1),
        'conv_w': nrm(ks[15], (DEPTH, CONV_WIDTH, C_WIDTH), CONV_WIDTH ** -0.5),
        'conv_b': nrm(ks[16], (DEPTH, C_WIDTH), 0.01),
        'conv_ln_gain': 1.0 + nrm(ks[17], (DEPTH, C_WIDTH), 0.05),
        'conv_ln_bias': nrm(ks[18], (DEPTH, C_WIDTH), 0.01),
        'w_out': nrm(ks[19], (DEPTH, D_MIX, D_MODEL), D_MIX ** -0.5),
        'w_up': nrm(ks[20], (DEPTH, D_MODEL, D_FF), D_MODEL ** -0.5),
        'w_down': nrm(ks[21], (DEPTH, D_FF, D_MODEL), D_FF ** -0.5),
    }


def reference(x_prompt, x_sample, cache_k, cache_v, cache_conv, norm_mix_pre, norm_mix_post,
              norm_mlp_pre, norm_mlp_post, w_in, diff_lambda, diff_subln, gmlp_norm, gmlp_w_s,
              gmlp_bias, conv_w, conv_b, conv_ln_gain, conv_ln_bias, w_out, w_up, w_down):
    xp = x_prompt
    xs = x_sample
    kp, vp, cp, ksl, vsl, csl, gsl = [], [], [], [], [], [], []
    for l in range(DEPTH):
        lw = (norm_mix_pre[l], norm_mix_post[l], norm_mlp_pre[l], norm_mlp_post[l], w_in[l],
              diff_lambda[l], diff_subln[l], gmlp_norm[l], gmlp_w_s[l], gmlp_bias[l], conv_w[l],
              conv_b[l], conv_ln_gain[l], conv_ln_bias[l], w_out[l], w_up[l], w_down[l])
        xp, k_p, v_p, c_p, _ = encoder_layer(xp, 0, l, None, None, None, lw)
        xs, k_s, v_s, c_s, g_s = encoder_layer(xs, PAST_LEN, l, cache_k[l], cache_v[l], cache_conv[l], lw)
        kp.append(k_p)
        vp.append(v_p)
        cp.append(c_p)
        ksl.append(k_s)
        vsl.append(v_s)
        csl.append(c_s)
        gsl.append(g_s)
    new_k_prompt = jnp.stack(kp)
    new_v_prompt = jnp.stack(vp)
    new_conv_prompt = jnp.stack(cp)
    new_k_sample = jnp.stack(ksl)
    new_v_sample = jnp.stack(vsl)
    new_conv_sample = jnp.stack(csl)
    new_gmlp_v_sample = jnp.stack(gsl)
    return (xp, xs, new_k_prompt, new_v_prompt, new_conv_prompt, new_k_sample, new_v_sample,
            new_conv_sample, new_gmlp_v_sample)
```

```python
import math
import contextlib
import numpy as np
import concourse.bass as bass
import concourse.mybir as mybir
from concourse.bass_utils import run_bass_kernel_spmd

F32 = mybir.dt.float32
BF16 = mybir.dt.bfloat16
AF = mybir.ActivationFunctionType
ALU = mybir.AluOpType
AX = mybir.AxisListType

D = 1024
DEPTH = 4
SEQ = 2048
NB = 4
DSEQ = 64
PAST = 4096
DIN = 2560
DFF = 4096
EPS = 1e-6
CW = 31


class Op:
    __slots__ = ("eng", "fn", "deps", "sig", "has_dep", "is_dma", "waits")

    def __init__(self, eng, fn, is_dma):
        self.eng = eng
        self.fn = fn
        self.deps = []
        self.sig = None
        self.has_dep = False
        self.is_dma = is_dma
        self.waits = []


class Sched:
    ENGS = ("pe", "act", "dve", "pool", "sp")
    NDSEM = 8
    GEN = 30000

    def __init__(self, nc):
        self.nc = nc
        self.ops = []
        self.per_eng = {e: [] for e in self.ENGS}
        self.last_w = {}
        self.readers = {}
        self.dma_count = {e: 0 for e in self.ENGS}
        self.dma_last = {}
        self.alias = {}
        self.expand = {}
        self._cap = None
        self._grp = None

    def begin_capture(self):
        self._cap = []
        self._grp = None

    def atomic_begin(self):
        if self._cap is not None:
            self._grp = []

    def atomic_end(self):
        if self._cap is not None and self._grp is not None:
            self._cap.append(self._grp)
            self._grp = None

    def end_capture(self):
        c = self._cap
        self._cap = None
        return c

    def replay_spread(self, lists):
        items = []
        for k, lst in enumerate(lists):
            n = len(lst)
            for i, it in enumerate(lst):
                items.append(((i + 0.5) / n, k, i, it))
        items.sort(key=lambda t: (t[0], t[1], t[2]))
        for _, _, _, item in items:
            if isinstance(item, list):
                for it in item:
                    self.add(*it)
            else:
                self.add(*item)

    def replay_rr(self, lists):
        idx = [0] * len(lists)
        left = sum(len(x) for x in lists)
        while left:
            for k, lst in enumerate(lists):
                if idx[k] < len(lst):
                    item = lst[idx[k]]
                    if isinstance(item, list):
                        for it in item:
                            self.add(*it)
                    else:
                        self.add(*item)
                    idx[k] += 1
                    left -= 1

    def barrier(self):
        lasts = [self.per_eng[e][-1] for e in self.ENGS if self.per_eng[e]]
        lasts += list(self.dma_last.values())
        for e in ("pe", "act", "dve", "pool", "sp"):
            self.add(e, lambda eng: eng.nop(), extra_deps=lasts)

    def add(self, eng, fn, reads=(), writes=(), dma=False, extra_deps=()):
        if self._cap is not None:
            rec = (eng, fn, list(reads), list(writes), dma, extra_deps)
            if self._grp is not None:
                self._grp.append(rec)
            else:
                self._cap.append(rec)
            return None
        op = Op(eng, fn, dma)
        deps = set(extra_deps)
        reads = [x for r in reads for x in self.expand.get(r, (self.alias.get(r, r),))]
        writes = [x for r in writes for x in self.expand.get(r, (self.alias.get(r, r),))]
        writes = writes + [r for r in reads if r.startswith("ps") and r not in writes]
        for r in reads:
            w = self.last_w.get(r)
            if w is not None:
                deps.add(w)
        for r in writes:
            w = self.last_w.get(r)
            if w is not None:
                deps.add(w)
            for rd in self.readers.get(r, ()):
                deps.add(rd)
        if dma:
            n = self.dma_count[eng]
            self.dma_count[eng] = n + 1
            j = n % self.NDSEM
            prev = self.dma_last.get((eng, j))
            if prev is not None:
                deps.add(prev)
            self.dma_last[(eng, j)] = op
            op.sig = (("d", eng, j), 16 * (n // self.NDSEM + 1))
        for d in deps:
            if d.eng == "pe" and eng == "pe" and not d.is_dma and not dma:
                continue
            d.has_dep = True
            op.deps.append(d)
        for r in reads:
            self.readers.setdefault(r, []).append(op)
        for r in writes:
            self.last_w[r] = op
            self.readers[r] = []
        self.ops.append(op)
        self.per_eng[eng].append(op)
        return op

    def finalize_and_emit(self):
        nc = self.nc
        cnt = {e: 0 for e in self.ENGS}
        G = self.GEN
        for op in self.ops:
            if op.is_dma:
                continue
            if op.has_dep:
                cnt[op.eng] += 1
                op.sig = (("e", op.eng), cnt[op.eng])
        clock = {e: {} for e in self.ENGS}
        for op in self.ops:
            ck = clock[op.eng]
            need = {}
            for d in op.deps:
                k, v = d.sig
                if ck.get(k, 0) >= v:
                    continue
                if need.get(k, 0) < v:
                    need[k] = v
            for k, v in need.items():
                ck[k] = v
                op.waits.append((k, v))
        keys = []
        for e in self.ENGS:
            for g in range((cnt[e] + G - 1) // G + 1):
                keys.append(("e", e, g))
        dma_final = {}
        for q in self.ENGS:
            n = self.dma_count[q]
            for j in range(min(self.NDSEM, n)):
                keys.append(("d", q, j))
                dma_final[("d", q, j)] = 16 * ((n - 1 - j) // self.NDSEM + 1)

        def semval(k, v):
            if k[0] == "d":
                return k, v
            return ("e", k[1], (v - 1) // G), (v - 1) % G + 1

        with contextlib.ExitStack() as st:
            sems = {}
            for k in keys:
                sems[k] = st.enter_context(nc.semaphore("s_" + "_".join(str(x) for x in k)))
            block = st.enter_context(nc.Block())

            def mk(ename):
                ops = self.per_eng[ename]

                def body(eng):
                    for op in ops:
                        for (k, v) in op.waits:
                            k2, v2 = semval(k, v)
                            eng.wait_ge(sems[k2], v2)
                        inst = op.fn(eng)
                        if op.is_dma:
                            inst.then_inc(sems[op.sig[0]], 16)
                        elif op.sig is not None:
                            k2, v2 = semval(*op.sig)
                            inst.then_inc(sems[k2], 1)
                    if ename == "sp":
                        for k, v in dma_final.items():
                            eng.wait_ge(sems[k], v)
                return body

            block.tensor(mk("pe"))
            block.scalar(mk("act"))
            block.vector(mk("dve"))
            block.gpsimd(mk("pool"))
            block.sync(mk("sp"))


def i_mm(out, lhsT, rhs, start=True, stop=True):
    return lambda e: e.matmul(out, lhsT, rhs, start=start, stop=stop)


def i_tr(out, in_, ident):
    return lambda e: e.transpose(out, in_, ident)


def i_act(out, in_, func, scale=None, bias=None, accum_out=None):
    kw = {}
    if scale is not None:
        kw["scale"] = scale
    if bias is not None:
        kw["bias"] = bias
    if accum_out is not None:
        kw["accum_out"] = accum_out
    return lambda e: e.activation(out, in_, func, **kw)


def i_ts(out, in0, s1, s2, op0, op1=None):
    if op1 is None:
        return lambda e: e.tensor_scalar(out, in0, s1, None, op0)
    return lambda e: e.tensor_scalar(out, in0, s1, s2, op0, op1)


def i_tt(out, in0, in1, op):
    return lambda e: e.tensor_tensor(out, in0, in1, op)


def i_stt(out, in0, scalar, in1, op0, op1):
    return lambda e: e.scalar_tensor_tensor(out, in0, scalar, in1, op0, op1)


def i_cp(out, in_):
    return lambda e: e.tensor_copy(out, in_)


def i_ms(ap, v):
    return lambda e: e.memset(ap, v)


def i_dma(out, in_, slow=False):
    if slow:
        return lambda e: e.dma_start(out=out, in_=in_, allow_slow_non_contiguous=True)
    return lambda e: e.dma_start(out=out, in_=in_)


def build_nc(n_prompt=NB, depth=DEPTH, do_sample=True):
    nc = bass.Bass("TRN2", target_bir_lowering=False, dynamic_dma_scratch_size=2048)
    S = Sched(nc)

    def din(name, shape):
        return nc.dram_tensor(name, list(shape), F32, kind="ExternalInput").ap()

    def dout(name, shape):
        return nc.dram_tensor(name, list(shape), F32, kind="ExternalOutput").ap()

    xp = din("x_prompt", [NB, SEQ, D])
    xs_d = din("x_sample", [NB, DSEQ, D])
    ck = din("cache_k", [DEPTH, NB, PAST, 512])
    cv = din("cache_v", [DEPTH, NB, PAST, 512])
    cc = din("cache_conv_t", [DEPTH, NB, 128, 2, CW - 1])
    gpre_t = din("norm_pre_t", [DEPTH, 128, 2, 8])
    g_mix_post = din("norm_mix_post", [DEPTH, D])
    g_mlp_post = din("norm_mlp_post", [DEPTH, D])
    w_in = din("w_in", [DEPTH, D, DIN])
    dlam = din("diff_lambda", [DEPTH, 256])
    dsub = din("diff_subln", [DEPTH, 128])
    gnorm = din("gmlp_norm", [DEPTH, 256])
    gws = din("gmlp_w_s", [DEPTH, 4, 128, 128])
    gbias = din("gmlp_bias_t", [DEPTH, 128, 4])
    conv_w = din("conv_w_t", [DEPTH, 128, 2, CW])
    conv_b = din("conv_b_t", [DEPTH, 128, 2])
    ln_g = din("conv_ln_gain", [DEPTH, 256])
    ln_b = din("conv_ln_bias", [DEPTH, 256])
    w_out = din("w_out", [DEPTH, D, D])
    w_up = din("w_up", [DEPTH, D, DFF])
    w_down = din("w_down", [DEPTH, DFF, D])
    c_ident = din("c_ident", [128, 128])
    c_tril = din("c_tril", [128, 128])
    c_rope_p = din("c_rope_p", [128, 16, 16])
    c_rope_s = din("c_rope_s", [64, 16])

    yp = dout("y_prompt", [NB, SEQ, D])
    ys = dout("y_sample", [NB, DSEQ, D])
    nkp = dout("new_k_prompt", [DEPTH, NB, SEQ, 512])
    nvp = dout("new_v_prompt", [DEPTH, NB, SEQ, 512])
    ncp = dout("new_conv_prompt", [DEPTH, NB, CW - 1, 256])
    nks = dout("new_k_sample", [DEPTH, NB, DSEQ, 512])
    nvs = dout("new_v_sample", [DEPTH, NB, DSEQ, 512])
    ncs = dout("new_conv_sample", [DEPTH, NB, CW - 1, 256])
    ngs = dout("new_gmlp_v_sample", [DEPTH, NB, DSEQ, 256])

    base0 = nc.sbuf_base
    top = nc.sbuf_top
    cur = [((base0 + 63) // 64) * 64]
    cnt_alloc = [0]

    def alloc(shape, dtype, at=None):
        nbytes = int(np.prod(shape[1:])) * (4 if dtype in (F32, mybir.dt.int32) else 2)
        nbytes = ((nbytes + 63) // 64) * 64
        if at is None:
            off = cur[0]
            cur[0] += nbytes
        else:
            off = at
        assert off + nbytes <= top, ("SBUF overflow", off, nbytes, top)
        cnt_alloc[0] += 1
        t = nc.alloc_sbuf_tensor_at("t%d" % cnt_alloc[0], list(shape), dtype, offset=off)
        return t.ap(), off, nbytes

    def A(shape, dtype):
        return alloc(shape, dtype)[0]

    x_off = cur[0]
    x_res = A([128, 16, D], F32)
    identf = A([128, 128], F32)
    identb = A([128, 128], BF16)
    tril = A([128, 128], F32)
    rope_p = A([128, 16, 16], F32)
    rope_s = A([128, 16], F32)
    gpreT = A([128, 2, 8], F32)
    subln_bc = A([128, 128], F32)
    gn_bc = A([128, 256], F32)
    lng_bc = A([128, 256], F32)
    lnb_bc = A([128, 256], F32)
    convwT = A([128, 2, CW], F32)
    convbT = A([128, 2], F32)
    gbiasT = A([128, 4], F32)
    wsT = A([128, 4, 128], BF16)
    lamq = A([128, 256], F32)
    sm = A([128, 64], F32)
    mhalf = A([128, 8], F32)
    stats = A([128, 256], F32)
    ov_base = cur[0]

    gpostA = A([128, D], F32)
    win = A([128, 8, DIN], BF16)
    wout = A([128, 8, D], BF16)
    kt_off = cur[0]
    KT = A([128, 4, SEQ], BF16)
    Vext = A([128, 16, 516], BF16)
    kv_end = cur[0]
    hTb = [A([128, 8, 128], BF16) for _ in range(2)]
    QT = A([128, 4, 512], BF16)
    catTb = [A([128, 8, 128], BF16)]
    cat_off = cur[0]
    cat = A([128, 4, D], BF16)
    gtmp_ws = [nc.alloc_sbuf_tensor_at("wsraw%d" % h, [128, 128], F32, offset=cat_off + h * 512).ap()
               for h in range(4)]
    for h in range(4):
        S.alias["wsraw%d" % h] = "cat0"
    xsbb = [A([128, D], BF16) for _ in range(2)]
    junk = A([128, 512], BF16)
    qr = A([128, 512], BF16)
    kf = A([128, 512], F32)
    kr = A([128, 512], BF16)
    vf = A([128, 512], F32)
    rtmpb = [A([128, 4, 64], F32) for _ in range(2)]
    vnb = A([128, 256], BF16)
    gluTg = A([128, 2, 30 + 512], F32)
    yTg = A([128, 2, 512], F32)
    ptmpP = A([128, 512], F32)
    Eb = [A([128, 512], BF16) for _ in range(2)]
    u0 = cur[0]
    Oe = A([128, 2, 4, 129], F32)
    at0 = A([128, 4, 128], F32)
    at1 = A([128, 4, 128], F32)
    ptmp = A([128, 512], F32)
    u1 = cur[0]
    cur[0] = u0
    gsq = A([128, 512], F32)
    gel = A([128, 512], F32)
    cur[0] = u0 + 4160
    vn = A([128, 256], F32)
    mtmp = A([128, 256], F32)
    glu = A([128, 256], F32)
    gtmp = A([128, 256], F32)
    cy = A([128, 256], F32)
    cy2 = A([128, 256], F32)
    assert cur[0] <= u1
    cur[0] = u1
    for nm in ("Oe0", "Oe1", "gsq", "gel"):
        S.alias[nm] = "U_a"
    for nm in ("at0", "vn", "mtmp"):
        S.alias[nm] = "U_b"
    for nm in ("at1", "glu", "gtmp"):
        S.alias[nm] = "U_c"
    for nm in ("ptmp", "cy", "cy2"):
        S.alias[nm] = "U_d"
    endA = cur[0]

    cur[0] = ov_base
    gpostB = A([128, D], F32)
    upT = A([128, 32, 1024], BF16)
    hT2_off = cur[0]
    hT2b = [A([128, 8, 1024], BF16) for _ in range(2)]
    f1 = A([128, 8, 512], F32)
    wupb = [A([128, 8, 512], BF16) for _ in range(2)]
    wdnb = [A([128, 4, 512], BF16) for _ in range(2)]
    rl = A([128, 512], BF16)
    rl_b = A([128, 512], BF16)
    xsb2 = A([128, D], BF16)
    junk2 = A([128, 512], BF16)
    ptmp2 = A([128, 512], F32)
    endB = cur[0]
    f0b = [nc.alloc_sbuf_tensor_at("f0alias%d" % k, [128, 8, 512], F32, offset=hT2_off + k * 16384).ap()
           for k in range(2)]

    cur[0] = kt_off
    KTs = A([128, 4, 256], BF16)
    Vs = A([128, 4, 516], BF16)
    Vc = [A([128, 8, 516], BF16) for _ in range(2)]
    assert cur[0] <= kv_end, (cur[0], kv_end)
    cur[0] = x_off + 4 * 4096
    kcb = [A([128, 8, 512], BF16) for _ in range(2)]
    KTc = [A([128, 4, 1024], BF16) for _ in range(2)]
    assert cur[0] <= x_off + 12 * 4096
    cur[0] = x_off + 12 * 4096
    Oacc = A([128, 8, 129], F32)
    cur[0] = max(endA, endB)
    print("SBUF used", cur[0], "of", top, "A", endA, "B", endB)

    ps_all = nc.alloc_psum_tensor("ps_all", [128, 8 * 512], F32).ap()
    ps = [ps_all[:, b * 512:(b + 1) * 512] for b in range(8)]
    psb = [p.bitcast(BF16) for p in ps]

    def PSR(b):
        return "ps%d" % b

    sp_add = lambda fn, r=(), w=(): S.add("sp", fn, r, w, dma=True)
    pool_dma = lambda fn, r=(), w=(): S.add("pool", fn, r, w, dma=True)

    sp_add(i_dma(identf, c_ident), w=["identf"])
    sp_add(i_dma(tril, c_tril), w=["tril"])
    sp_add(i_dma(rope_p, c_rope_p), w=["rope_p"])
    sp_add(i_dma(rope_s[0:64, :], c_rope_s), w=["rope_s"])
    S.add("dve", i_cp(identb, identf), ["identf"], ["identb"])
    S.add("pool", i_ms(mhalf, -0.5), (), ["mhalf"])

    stat_ctr = [0, 0]
    stat_pool = [0]

    def new_stat(n=1):
        p = stat_pool[0]
        k = p * 16 + stat_ctr[p] % 16
        stat_ctr[p] += 1
        return stats[:, k * 8:k * 8 + n], "stat%d" % k

    def rsqrt_mean(P, ssum_ap, ssum_res, n, width, scale_after=None):
        st, res = new_stat(width)
        st2, res2 = new_stat(width)
        S.add("pool", i_ts(st[:P], ssum_ap, 1.0 / n, EPS, ALU.mult, ALU.add), [ssum_res], [res])
        S.add("pool", i_tt(st2[:P], st[:P], mhalf[:P, 0:width], ALU.pow), [res, "mhalf"], [res2])
        return st2, res2

    def load_layer_consts(l, P):
        lam_init = 0.8 - 0.6 * math.exp(-0.3 * l)
        sp_add(i_dma(gpostA, g_mix_post[l].partition_broadcast(128), slow=True), w=["gpostA"])
        sp_add(i_dma(gpreT, gpre_t[l]), w=["gpreT"])
        sp_add(i_dma(subln_bc, dsub[l].partition_broadcast(128), slow=True), w=["subln"])
        sp_add(i_dma(gn_bc, gnorm[l].partition_broadcast(128), slow=True), w=["gn"])
        sp_add(i_dma(lng_bc, ln_g[l].partition_broadcast(128), slow=True), w=["lng"])
        sp_add(i_dma(lnb_bc, ln_b[l].partition_broadcast(128), slow=True), w=["lnb"])
        sp_add(i_dma(convwT, conv_w[l]), w=["convw"])
        sp_add(i_dma(convbT, conv_b[l]), w=["convb"])
        sp_add(i_dma(gbiasT, gbias[l]), w=["gbias"])
        sp_add(i_dma(lamq, dlam[l].partition_broadcast(128), slow=True), w=["lamq"])
        S.add("pool", i_ts(subln_bc, subln_bc, 1.0 - lam_init, 0.0, ALU.mult, ALU.add), ["subln"], ["subln"])
        S.add("dve", i_tt(lamq[:, 0:64], lamq[:, 0:64], lamq[:, 64:128], ALU.mult), ["lamq"], ["lamq"])
        S.add("dve", i_tt(lamq[:, 128:192], lamq[:, 128:192], lamq[:, 192:256], ALU.mult), ["lamq"], ["lamq"])
        S.add("dve", lambda e: e.tensor_reduce(sm[:, 0:1], lamq[:, 0:64], AX.X, ALU.add), ["lamq"], ["sm"])
        S.add("dve", lambda e: e.tensor_reduce(sm[:, 1:2], lamq[:, 128:192], AX.X, ALU.add), ["lamq"], ["sm"])
        S.add("act", i_act(sm[:, 2:4], sm[:, 0:2], AF.Exp), ["sm"], ["sm2"])
        S.add("dve", i_tt(sm[:, 4:5], sm[:, 3:4], sm[:, 2:3], ALU.subtract), ["sm2"], ["sm3"])
        S.add("dve", i_ts(sm[:, 5:6], sm[:, 4:5], -lam_init, None, ALU.add), ["sm3"], ["nlam"])
        for h in range(4):
            sp_add(i_dma(gtmp_ws[h], gws[l, h]), w=["wsraw%d" % h])
            S.add("dve", i_tt(gtmp_ws[h], gtmp_ws[h], tril, ALU.mult), ["wsraw%d" % h, "tril"], ["wsraw%d" % h])
            S.add("pe", i_tr(ps[7][:, h * 128:(h + 1) * 128], gtmp_ws[h], identf), ["wsraw%d" % h, "identf"], [PSR(7)])
        S.add("act", i_act(wsT.rearrange("p h t -> p (h t)"), ps[7][:, 0:512], AF.Copy), [PSR(7)], ["wsT"])

    def load_w_in_out(l):
        wi = w_in[l].rearrange("(c p) n -> p c n", p=128)
        for cg in range(5):
            pool_dma(i_dma(win[:, :, cg * 512:(cg + 1) * 512], wi[:, :, cg * 512:(cg + 1) * 512]),
                     w=["win%d" % cg])
        wo = w_out[l].rearrange("(c p) n -> p c n", p=128)
        for dh in range(2):
            pool_dma(i_dma(wout[:, :, dh * 512:(dh + 1) * 512], wo[:, :, dh * 512:(dh + 1) * 512]),
                     w=["wout%d" % dh])

    def norm_to_hT(P, xt, xres, which, xsb_, xsb_res, junk_, junk_res, dst, dst_res, dcol, tpbank):
        ss, ssr = new_stat(1)
        S.add("act", i_act(junk_[:P], xt, AF.Square, accum_out=ss[:P]), [xres], [junk_res, ssr])
        rstd, rr = rsqrt_mean(P, ss[:P], ssr, D, 1)
        S.add("dve", i_ts(xsb_[:P], xt, rstd[:P], None, ALU.mult), [xres, rr], [xsb_res])
        S.atomic_begin()
        for c in range(8):
            S.add("pe", i_tr(psb[tpbank][:, c * P:(c + 1) * P], xsb_[:P, c * 128:(c + 1) * 128], identb[:P, :P]),
                  [xsb_res, "identb"], ["ps0"])
        S.add("dve", i_tt(dst[:, :, dcol:dcol + P],
                          psb[tpbank][:, 0:8 * P].rearrange("p (c t) -> p c t", c=8),
                          gpreT[:, which, :].unsqueeze(2).to_broadcast([128, 8, P]), ALU.mult),
              ["ps0", "gpreT"], [dst_res])
        S.atomic_end()

    def post_norm_residual(P, xt, xres, which, banks, ptmp_, ptmp_res, junk_, junk_res, srcs=None):
        if srcs is None:
            srcs = [(ps[banks[0]][:P, :], PSR(banks[0])), (ps[banks[1]][:P, :], PSR(banks[1]))]
        ss, ssr = new_stat(2)
        for dh in range(2):
            S.add("act", i_act(junk_[:P, 0:512], srcs[dh][0], AF.Square, accum_out=ss[:P, dh:dh + 1]),
                  [srcs[dh][1]], [junk_res, ssr])
        st, sr = new_stat(1)
        S.add("pool", i_tt(st[:P], ss[:P, 0:1], ss[:P, 1:2], ALU.add), [ssr], [sr])
        rstd, rr = rsqrt_mean(P, st[:P], sr, D, 1)
        for dh in range(2):
            gp = gpostA if which == 0 else gpostB
            gpr = "gpostA" if which == 0 else "gpostB"
            S.add("dve", i_stt(ptmp_[:P], srcs[dh][0], rstd[:P], gp[:P, dh * 512:(dh + 1) * 512],
                               ALU.mult, ALU.mult), [srcs[dh][1], rr, gpr], [ptmp_res])
            S.add("pool", i_tt(xt[:, dh * 512:(dh + 1) * 512], xt[:, dh * 512:(dh + 1) * 512], ptmp_[:P], ALU.add),
                  [xres, ptmp_res], [xres])

    def rope(P, zb_, zres, cs, cs_res, dstf, dst_res, rk=0):
        z3 = zb_.rearrange("p (h d) -> p h d", h=8)
        d3 = dstf.rearrange("p (h d) -> p h d", h=8)
        cosb = cs[:, 0:8].unsqueeze(1).to_broadcast([P, 8, 8])
        sinb = cs[:, 8:16].unsqueeze(1).to_broadcast([P, 8, 8])
        rtmp = rtmpb[rk]
        t = [rtmp[:P, k, :].rearrange("p (h d) -> p h d", h=8) for k in range(4)]
        x1 = z3[:, :, 0:8]
        x2 = z3[:, :, 8:16]
        zr = [zres]
        S.add("dve", i_tt(t[0], x1, cosb, ALU.mult), zr + [cs_res], ["rt0_%d" % rk])
        S.add("dve", i_tt(t[1], x2, sinb, ALU.mult), zr + [cs_res], ["rt1_%d" % rk])
        S.add("dve", i_tt(t[2], x2, cosb, ALU.mult), zr + [cs_res], ["rt2_%d" % rk])
        S.add("dve", i_tt(t[3], x1, sinb, ALU.mult), zr + [cs_res], ["rt3_%d" % rk])
        S.add("dve", i_tt(d3[:, :, 0:8], t[0], t[1], ALU.subtract), ["rt0_%d" % rk, "rt1_%d" % rk], [dst_res])
        S.add("dve", i_tt(d3[:, :, 8:16], t[2], t[3], ALU.add), ["rt2_%d" % rk, "rt3_%d" % rk], [dst_res])
        S.add("act", i_act(d3[:, :, 16:64], z3[:, :, 16:64], AF.Copy), zr, [dst_res])

    def diff_combine(P, O0, O1, ores, out4, out_res):
        rr, rres = new_stat(8)
        S.add("dve", lambda e: e.reciprocal(rr[:P, 0:4], O0[:, :, 128]), ores, [rres])
        S.add("dve", lambda e: e.reciprocal(rr[:P, 4:8], O1[:, :, 128]), ores, [rres])
        S.add("dve", i_ts(rr[:P, 4:8], rr[:P, 4:8], sm[:P, 5:6], None, ALU.mult), [rres, "nlam"], [rres])
        S.add("dve", i_tt(at0[:P], O0[:, :, 0:128], rr[:P, 0:4].unsqueeze(2).to_broadcast([P, 4, 128]), ALU.mult),
              ores + [rres], ["at0"])
        S.add("dve", i_tt(at1[:P], O1[:, :, 0:128], rr[:P, 4:8].unsqueeze(2).to_broadcast([P, 4, 128]), ALU.mult),
              ores + [rres], ["at1"])
        S.add("dve", i_tt(at0[:P], at0[:P], at1[:P], ALU.add), ["at0", "at1"], ["at0"])
        S.add("dve", i_tt(at1[:P], at0[:P], at0[:P], ALU.mult), ["at0"], ["at1"])
        ss, ssr = new_stat(4)
        S.add("dve", lambda e: e.tensor_reduce(ss[:P], at1[:P], AX.X, ALU.add), ["at1"], [ssr])
        rstd, rsr = rsqrt_mean(P, ss[:P], ssr, 128, 4)
        S.add("dve", i_tt(at0[:P], at0[:P], rstd[:P].unsqueeze(2).to_broadcast([P, 4, 128]), ALU.mult),
              ["at0", rsr], ["at0"])
        S.add("dve", i_tt(out4, at0[:P], subln_bc[:P].unsqueeze(1).to_broadcast([P, 4, 128]), ALU.mult),
              ["at0", "subln"], list(out_res))

    def front_N(P, xt, xres, par):
        norm_to_hT(P, xt, xres, 0, xsbb[par], "xsb%d" % par, xsbb[par], "xsb%d" % par,
                   hTb[par], "hT%d" % par, 0, 0)

    def front_M(P, par):
        hT = hTb[par]
        for cg in range(5):
            for c in range(8):
                S.add("pe", i_mm(ps[1 + cg][:P, :], hT[:, c, 0:P], win[:, c, cg * 512:(cg + 1) * 512],
                                 start=(c == 0), stop=(c == 7)), ["hT%d" % par, "win%d" % cg], [PSR(1 + cg)])

    def tile_front(l, P, xt, xres, il, hcol, cs, cs_res, kt_dst, kt_res, v_dst, v_res, cat_t, cat_res,
                   k_out, v_out, gcol, conv_out, gv_out, extra_caps=(), pre_caps=()):
        caps = list(pre_caps)
        S.begin_capture()
        rope(P, ps[1][:P, :], PSR(1), cs, cs_res, qr[:P], "qr", rk=0)
        S.atomic_begin()
        for j in range(4):
            S.add("pe", i_tr(psb[0][:, j * P:(j + 1) * P], qr[:P, j * 128:(j + 1) * 128], identb[:P, :P]),
                  ["qr", "identb"], ["ps0"])
        S.add("dve", i_cp(QT[:, :, hcol:hcol + P], psb[0][:, 0:4 * P].rearrange("p (c t) -> p c t", c=4)),
              ["ps0"], ["QT"])
        S.atomic_end()
        caps.append(S.end_capture())
        S.begin_capture()
        rope(P, ps[2][:P, :], PSR(2), cs, cs_res, kf[:P], "kf", rk=1)
        sp_add(i_dma(k_out, kf[:P]), r=["kf"])
        S.add("pool", i_cp(kr[:P], kf[:P]), ["kf"], ["kr"])
        S.atomic_begin()
        for j in range(4):
            S.add("pe", i_tr(psb[0][:, 512 + j * P:512 + (j + 1) * P], kr[:P, j * 128:(j + 1) * 128], identb[:P, :P]),
                  ["kr", "identb"], ["ps0"])
        S.add("dve", i_cp(kt_dst, psb[0][:, 512:512 + 4 * P].rearrange("p (c t) -> p c t", c=4)), ["ps0"], [kt_res])
        S.atomic_end()
        caps.append(S.end_capture())
        S.begin_capture()
        S.add("act", i_act(vf[:P], ps[3][:P, :], AF.Copy), [PSR(3)], ["vf"])
        sp_add(i_dma(v_out, vf[:P]), r=["vf"])
        S.add("pool", i_cp(v_dst, vf[:P].rearrange("p (h e) -> p h e", h=4)), ["vf"], [v_res])
        caps.append(S.end_capture())
        S.begin_capture()
        zb = ps[4][:P, :]
        S.add("act", i_act(gel[:P], zb, AF.Copy), [PSR(4)], ["gel"])
        S.add("act", i_act(gsq[:P], gel[:P], AF.Square), ["gel"], ["gsq"])
        S.add("dve", i_ts(gsq[:P], gsq[:P], 0.044715, 1.0, ALU.mult, ALU.add), ["gsq"], ["gsq"])
        S.add("dve", i_tt(gsq[:P], gsq[:P], gel[:P], ALU.mult), ["gsq", "gel"], ["gsq"])
        S.add("act", i_act(gsq[:P], gsq[:P], AF.Tanh, scale=0.7978845608028654), ["gsq"], ["gsq"])
        S.add("dve", i_ts(gsq[:P], gsq[:P], 0.5, 0.5, ALU.mult, ALU.add), ["gsq"], ["gsq"])
        S.add("dve", i_tt(gel[:P], gsq[:P], gel[:P], ALU.mult), ["gsq", "gel"], ["gel"])
        ss, ssr = new_stat(1)
        S.add("act", i_act(mtmp[:P], gel[:P, 256:512], AF.Square, accum_out=ss[:P]), ["gel"], ["mtmp", ssr])
        rstd, rr = rsqrt_mean(P, ss[:P], ssr, 256, 1)
        S.add("dve", i_stt(vn[:P], gel[:P, 256:512], rstd[:P], gn_bc[:P], ALU.mult, ALU.mult),
              ["gel", rr, "gn"], ["vn"])
        if gv_out is not None:
            sp_add(i_dma(gv_out, vn[:P]), r=["vn"])
        S.add("act", i_act(vnb[:P], vn[:P], AF.Copy), ["vn"], ["vnb"])
        S.atomic_begin()
        for h in range(4):
            S.add("pe", i_mm(ps[6][:P, h * 64:(h + 1) * 64], wsT[:P, h, :P], vnb[:P, h * 64:(h + 1) * 64]),
                  ["wsT", "vnb"], [PSR(6)])
        S.add("dve", i_tt(mtmp[:P].rearrange("p (h d) -> p h d", h=4),
                          ps[6][:P, 0:256].rearrange("p (h d) -> p h d", h=4),
                          gbiasT[:P].unsqueeze(2).to_broadcast([P, 4, 64]), ALU.add),
              [PSR(6), "gbias"], ["mtmp"])
        S.atomic_end()
        S.add("dve", i_tt(cat_t[:, 512:768], mtmp[:P], gel[:P, 0:256], ALU.mult), ["mtmp", "gel"], [cat_res])
        caps.append(S.end_capture())
        S.begin_capture()
        zc = ps[5][:P, :]
        S.add("act", i_act(gtmp[:P], zc[:, 256:512], AF.Tanh, scale=0.5), [PSR(5)], ["gtmp"])
        S.add("act", i_act(glu[:P], zc[:, 0:256], AF.Copy), [PSR(5)], ["glu"])
        S.add("dve", i_ts(gtmp[:P], gtmp[:P], 0.5, 0.5, ALU.mult, ALU.add), ["gtmp"], ["gtmp"])
        S.add("dve", i_tt(glu[:P], gtmp[:P], glu[:P], ALU.mult), ["gtmp", "glu"], ["glu"])
        if conv_out is not None:
            sp_add(i_dma(conv_out, glu[P - 30:P, :]), r=["glu"])
        S.atomic_begin()
        for c2 in range(2):
            S.add("pe", i_tr(ps[7][:, c2 * P:(c2 + 1) * P], glu[:P, c2 * 128:(c2 + 1) * 128], identf[:P, :P]),
                  ["glu", "identf"], [PSR(7)])
        S.add("act", i_act(gluTg[:, :, 30 + gcol:30 + gcol + P], ps[7][:, 0:2 * P].rearrange("p (c t) -> p c t", c=2),
                           AF.Copy), [PSR(7)], ["gluTg"])
        S.atomic_end()
        caps.append(S.end_capture())
        caps.extend(extra_caps)
        S.replay_rr(caps)

    def conv_hist(mode, src=None, W=512):
        if mode == "zero":
            S.add("pool", i_ms(gluTg[:, :, 0:30], 0.0), (), ["gluTg"])
        elif mode == "carry":
            S.add("pool", i_cp(gluTg[:, :, 0:30], gluTg[:, :, W:W + 30]), ["gluTg"], ["gluTg"])
        else:
            sp_add(i_dma(gluTg[:, :, 0:30], src), w=["gluTg"])

    def conv_taps(W):
        S.add("dve", i_ts(yTg[:, 0, 0:W], gluTg[:, 0, 0:W], convwT[:, 0, 0:1], convbT[:, 0:1], ALU.mult, ALU.add),
              ["gluTg", "convw", "convb"], ["yT0"])
        S.add("pool", i_ts(yTg[:, 1, 0:W], gluTg[:, 1, 0:W], convwT[:, 1, 0:1], convbT[:, 1:2], ALU.mult, ALU.add),
              ["gluTg", "convw", "convb"], ["yT1"])
        for j in range(1, CW):
            S.add("dve", i_stt(yTg[:, 0, 0:W], gluTg[:, 0, j:j + W], convwT[:, 0, j:j + 1], yTg[:, 0, 0:W],
                               ALU.mult, ALU.add), ["gluTg", "convw", "yT0"], ["yT0"])
            S.add("act", i_act(ptmpP[:, 0:W], gluTg[:, 1, j:j + W], AF.Copy, scale=convwT[:, 1, j:j + 1]),
                  ["gluTg", "convw"], ["ptmpP"])
            S.add("pool", i_tt(yTg[:, 1, 0:W], yTg[:, 1, 0:W], ptmpP[:, 0:W], ALU.add), ["ptmpP", "yT1"], ["yT1"])

    def conv_post(P, gcol, cat_t, cat_res):
        S.atomic_begin()
        for c2 in range(2):
            S.add("pe", i_tr(ps[7][:P, c2 * 128:(c2 + 1) * 128], yTg[:, c2, gcol:gcol + P], identf),
                  ["yT%d" % c2, "identf"], [PSR(7)])
        st6, s6r = new_stat(6)
        mv, mvr = new_stat(2)
        S.add("dve", lambda e: e.bn_stats(st6[:P], ps[7][:P, 0:256]), [PSR(7)], [s6r])
        S.add("dve", lambda e: e.bn_aggr(mv[:P], st6[:P]), [s6r], [mvr])
        rstd, rr = rsqrt_mean(P, mv[:P, 1:2], mvr, 1, 1)
        S.add("dve", i_ts(cy[:P], ps[7][:P, 0:256], mv[:P, 0:1], rstd[:P], ALU.subtract, ALU.mult),
              [PSR(7), mvr, rr], ["cy"])
        S.atomic_end()
        S.add("dve", i_tt(cy[:P], cy[:P], lng_bc[:P], ALU.mult), ["cy", "lng"], ["cy"])
        S.add("dve", i_tt(cy[:P], cy[:P], lnb_bc[:P], ALU.add), ["cy", "lnb"], ["cy"])
        S.add("act", i_act(cy2[:P], cy[:P], AF.Tanh, scale=0.5), ["cy"], ["cy2"])
        S.add("dve", i_ts(cy2[:P], cy2[:P], 0.5, 0.5, ALU.mult, ALU.add), ["cy2"], ["cy2"])
        S.add("dve", i_tt(cat_t[:, 768:1024], cy2[:P], cy[:P], ALU.mult), ["cy2", "cy"], [cat_res])

    def tile_back(l, P, xt, xres, il, hcol, cat_t, cat_res, par=0):
        cT = catTb[par]
        cres = "catT%d" % par
        S.atomic_begin()
        for c in range(8):
            S.add("pe", i_tr(psb[0][:, c * P:(c + 1) * P], cat_t[:, c * 128:(c + 1) * 128], identb[:P, :P]),
                  [cat_res, "identb"], ["ps0"])
        S.add("act", i_act(cT[:, :, 0:P], psb[0][:, 0:8 * P].rearrange("p (c t) -> p c t", c=8), AF.Copy),
              ["ps0"], [cres])
        for dh in range(2):
            for c in range(8):
                S.add("pe", i_mm(ps[6 + dh][:P, :], cT[:, c, 0:P], wout[:, c, dh * 512:(dh + 1) * 512],
                                 start=(c == 0), stop=(c == 7)), [cres, "wout%d" % dh], [PSR(6 + dh)])
        post_norm_residual(P, xt, xres, 0, (6, 7), ptmp, "ptmp", junk, "junk")
        S.atomic_end()

    def mlp_norm(l, P, tiles, hb):
        for ti, (xt, xres) in enumerate(tiles):
            norm_to_hT(P, xt, xres, 1, xsb2, "xsb2", xsb2, "xsb2", hT2b[hb], "hT2_%d" % hb, ti * P, 0)

    up_rot = [0]

    def mlp_up(l, P, n, hb):
        W = n * P
        hT2 = hT2b[hb]
        wu = w_up[l].rearrange("(c p) n -> p c n", p=128)
        nsub = (W + 511) // 512

        def load_up(cg):
            pool_dma(i_dma(wupb[cg % 2], wu[:, :, cg * 512:(cg + 1) * 512]), w=["wup%d" % (cg % 2)])

        load_up(0)
        for cg in range(8):
            if cg + 1 < 8:
                load_up(cg + 1)
            for jj in range(4):
                j = cg * 4 + jj
                for sb in range(nsub):
                    c0 = sb * 512
                    wdt = min(512, W - c0)
                    b = 1 + (up_rot[0] % 4)
                    up_rot[0] += 1
                    for c in range(8):
                        S.add("pe", i_mm(ps[b][:, 0:wdt], wupb[cg % 2][:, c, jj * 128:(jj + 1) * 128],
                                         hT2[:, c, c0:c0 + wdt], start=(c == 0), stop=(c == 7)),
                              ["wup%d" % (cg % 2), "hT2_%d" % hb], [PSR(b)])
                    rb = "rl%d" % (up_rot[0] % 2)
                    rlt = rl if (up_rot[0] % 2) else rl_b
                    S.add("act", i_act(rlt[:, 0:wdt], ps[b][:, 0:wdt], AF.Relu), [PSR(b)], [rb])
                    S.add("dve", i_tt(upT[:, j, c0:c0 + wdt], rlt[:, 0:wdt], rlt[:, 0:wdt], ALU.mult), [rb], ["upT"])

    def mlp_down(l, P, n, hb):
        wd = w_down[l].rearrange("(j p) d -> p j d", p=128)

        def load_dn(k):
            dh, jq = divmod(k, 8)
            pool_dma(i_dma(wdnb[k % 2], wd[:, jq * 4:(jq + 1) * 4, dh * 512:(dh + 1) * 512]), w=["wdn%d" % (k % 2)])

        load_dn(0)
        for dh in range(2):
            for jq in range(8):
                k = dh * 8 + jq
                if k + 1 < 16:
                    load_dn(k + 1)
                for jj in range(4):
                    j = jq * 4 + jj
                    for ti in range(n):
                        S.add("pe", i_mm(ps[ti][:P, :], upT[:, j, ti * P:(ti + 1) * P], wdnb[k % 2][:, jj, :],
                                         start=(j == 0), stop=(j == 31)), ["upT", "wdn%d" % (k % 2)], [PSR(ti)])
            for ti in range(n):
                if dh == 0:
                    S.add("act", i_act(f0b[hb][:P, ti, :], ps[ti][:P, :], AF.Copy), [PSR(ti)], ["hT2_%d" % hb])
                elif ti % 2 == 0:
                    S.add("act", i_act(f1[:P, ti, :], ps[ti][:P, :], AF.Copy), [PSR(ti)], ["f1"])
                else:
                    S.add("dve", i_cp(f1[:P, ti, :], ps[ti][:P, :]), [PSR(ti)], ["f1"])

    def mlp_post(l, P, tiles, hb):
        for ti, (xt, xres) in enumerate(tiles):
            post_norm_residual(P, xt, xres, 1, None, ptmp2, "ptmp2", junk2, "junk2",
                               srcs=[(f0b[hb][:P, ti, :], "hT2_%d" % hb), (f1[:P, ti, :], "f1")])

    def mlp_phase(l, P, groups):
        sp_add(i_dma(gpostB, g_mlp_post[l].partition_broadcast(128), slow=True), w=["gpostB"])
        mlp_norm(l, P, groups[0], 0)
        for k, tiles in enumerate(groups):
            hb = k % 2
            caps = []
            S.begin_capture()
            mlp_up(l, P, len(tiles), hb)
            caps.append(S.end_capture())
            if k + 1 < len(groups):
                S.begin_capture()
                mlp_norm(l, P, groups[k + 1], 1 - hb)
                caps.append(S.end_capture())
            if k > 0:
                S.begin_capture()
                stat_pool[0] = 1
                mlp_post(l, P, groups[k - 1], 1 - hb)
                stat_pool[0] = 0
                caps.append(S.end_capture())
            S.replay_rr(caps)
            mlp_down(l, P, len(tiles), hb)
        mlp_post(l, P, groups[-1], (len(groups) - 1) % 2)

    def prompt_attention(g):
        its = []
        for h in range(4):
            for m in range(2):
                for j in range(4 * g + 4):
                    its.append((h, m, j))

        def emit_S(k):
            h, m, j = its[k]
            hs = 2 * h + m
            r0 = (hs % 2) * 64
            n0 = max(0, j - 4 * g) * 128
            sb = 1 + (k % 2)
            S.add("pe", i_mm(ps[sb][:, n0:512], KT[r0:r0 + 64, h, j * 128:(j + 1) * 128],
                             QT[r0:r0 + 64, h, n0:512]), ["KT", "QT"], [PSR(sb)])

        emit_S(0)
        for k, (h, m, j) in enumerate(its):
            il0 = max(0, j - 4 * g)
            n0 = il0 * 128
            sb = 1 + (k % 2)
            eb = k % 2
            S.add("act", i_act(Eb[eb][:, n0:512], ps[sb][:, n0:512], AF.Exp, scale=0.125),
                  [PSR(sb)], ["E%d" % eb])
            if j >= 4 * g:
                S.add("act", i_act(Eb[eb][64:128, n0:n0 + 64], Eb[eb][64:128, n0:n0 + 64], AF.Copy, scale=0.0),
                      ["E%d" % eb], ["E%d" % eb])
            if k + 1 < len(its):
                emit_S(k + 1)
            for il in range(il0, 4):
                S.add("pe", i_mm(ps[3 + il][:, 0:129], Eb[eb][:, il * 128:(il + 1) * 128],
                                 Vext[:, j, h * 129:(h + 1) * 129], start=(j == 0), stop=(j == 4 * g + il)),
                      ["E%d" % eb, "Vext"], [PSR(3 + il)])
            if j == 4 * g + 3:
                S.add("dve", i_cp(Oe[:, m], ps_all[:, 3 * 512:7 * 512].rearrange("p (b n) -> p b n", b=4)[:, :, 0:129]),
                      [PSR(3), PSR(4), PSR(5), PSR(6)], ["Oe%d" % m])
                if m == 1:
                    diff_combine(128, Oe[:, 0], Oe[:, 1], ["Oe0", "Oe1"], cat[:, :, h * 128:(h + 1) * 128],
                                     ["cat0", "cat1", "cat2", "cat3"])

    def sample_attention(l, s):
        P = 64
        ckl = ck[l, s].rearrange("(kt p) f -> p kt f", p=128)
        cvl = cv[l, s].rearrange("(kt p) f -> p kt f", p=128)

        def load_chunk(ch):
            bf = ch % 2
            pool_dma(i_dma(kcb[bf], ckl[:, ch * 8:(ch + 1) * 8, :]), w=["kcb%d" % bf])
            for h4 in range(4):
                pool_dma(i_dma(Vc[bf][:, :, h4 * 129:h4 * 129 + 128],
                               cvl[:, ch * 8:(ch + 1) * 8, h4 * 128:(h4 + 1) * 128]),
                         w=["Vc%d" % bf])

        load_chunk(0)
        sbank = 0
        for ch in range(5):
            bf = ch % 2
            if ch + 1 < 4:
                load_chunk(ch + 1)
            if ch < 4:
                for kp in range(4):
                    tb = 0 if kp % 2 == 0 else 7
                    for hp in range(4):
                        for kl in range(2):
                            kt = kp * 2 + kl
                            S.add("pe", i_tr(psb[tb][:, (hp * 2 + kl) * 128:(hp * 2 + kl + 1) * 128],
                                             kcb[bf][:, kt, hp * 128:(hp + 1) * 128], identb),
                                  ["kcb%d" % bf, "identb"], [PSR(tb)])
                    S.add("dve", i_cp(KTc[bf][:, :, kp * 256:(kp + 1) * 256],
                                      psb[tb][:, 0:1024].rearrange("p (c t) -> p c t", c=4)), [PSR(tb)], ["KTc%d" % bf])
            for hs in range(8):
                r0 = (hs % 2) * 64
                hp = hs // 2
                sb = 1 + (sbank % 2)
                eb = sbank % 2
                sbank += 1
                ob = 3 + hs // 3
                oc = (hs % 3) * 160
                if ch < 4:
                    for kt in range(8):
                        S.add("pe", i_mm(ps[sb][:, kt * 64:(kt + 1) * 64], KTc[bf][r0:r0 + 64, hp, kt * 128:(kt + 1) * 128],
                                         QT[r0:r0 + 64, hp, s * 64:(s + 1) * 64]), ["KTc%d" % bf, "QT"], [PSR(sb)])
                    S.add("act", i_act(Eb[eb][:, :], ps[sb][:, :], AF.Exp, scale=0.125), [PSR(sb)], ["E%d" % eb])
                    for kt in range(8):
                        S.add("pe", i_mm(ps[ob][:64, oc:oc + 129], Eb[eb][:, kt * 64:(kt + 1) * 64],
                                         Vc[bf][:, kt, hp * 129:(hp + 1) * 129], start=(kt == 0), stop=(kt == 7)),
                              ["E%d" % eb, "Vc%d" % bf], [PSR(ob)])
                else:
                    S.add("pe", i_mm(ps[sb][:64, 0:64], KTs[r0:r0 + 64, hp, s * 64:(s + 1) * 64],
                                     QT[r0:r0 + 64, hp, s * 64:(s + 1) * 64]), ["KTs", "QT"], [PSR(sb)])
                    S.add("act", i_act(Eb[eb][:64, 0:64], ps[sb][:64, 0:64], AF.Exp, scale=0.125), [PSR(sb)], ["E%d" % eb])
                    S.add("pe", i_mm(ps[ob][:64, oc:oc + 129], Eb[eb][:64, 0:64],
                                     Vs[:64, s, hp * 129:(hp + 1) * 129]), ["E%d" % eb, "Vs"], [PSR(ob)])
            for b3 in range(3):
                nh = 3 if b3 < 2 else 2
                src = ps[3 + b3][:64, 0:480].rearrange("p (a n) -> p a n", a=3)[:, 0:nh, 0:129]
                dst = Oacc[:64, 3 * b3:3 * b3 + nh, :]
                if ch == 0:
                    S.add("dve", i_cp(dst, src), [PSR(3 + b3)], ["Oacc"])
                else:
                    S.add("dve", i_tt(dst, dst, src, ALU.add), [PSR(3 + b3), "Oacc"], ["Oacc"])
        O4 = Oacc[:64].rearrange("p (h m) n -> p h m n", m=2)
        diff_combine(64, O4[:, :, 0, :], O4[:, :, 1, :], ["Oacc"],
                     cat[:64, s, 0:512].rearrange("p (h e) -> p h e", h=4), ["cat%d" % s])

    for b in range(n_prompt):
        for i in range(16):
            sp_add(i_dma(x_res[:, i, :], xp[b, i * 128:(i + 1) * 128, :]), w=["x%d" % i])
        for l in range(depth):
            S.add("pool", i_ms(Vext.rearrange("p t (h e) -> p t h e", h=4)[:, :, :, 128:129], 1.0), (), ["Vext"])
            load_layer_consts(l, 128)
            load_w_in_out(l)
            front_N(128, x_res[:, 0, :], "x0", 0)
            for g in range(4):
                conv_hist("zero" if g == 0 else "carry")
                for il in range(4):
                    i = 4 * g + il
                    front_M(128, i % 2)
                    pre = []
                    if g > 0:
                        ib = 4 * (g - 1) + il
                        S.begin_capture()
                        tile_back(l, 128, x_res[:, ib, :], "x%d" % ib, il, il * 128, cat[:, il, :], "cat%d" % il)
                        pre.append(S.end_capture())
                    extra = []
                    if i + 1 < 16:
                        S.begin_capture()
                        front_N(128, x_res[:, i + 1, :], "x%d" % (i + 1), (i + 1) % 2)
                        extra.append(S.end_capture())
                    tile_front(l, 128, x_res[:, i, :], "x%d" % i, il, il * 128, rope_p[:, i, :], "rope_p",
                               KT[:, :, i * 128:(i + 1) * 128], "KT",
                               Vext[:, i, :].rearrange("p (h e) -> p h e", h=4)[:, :, 0:128], "Vext",
                               cat[:, il, :], "cat%d" % il,
                               nkp[l, b, i * 128:(i + 1) * 128, :], nvp[l, b, i * 128:(i + 1) * 128, :],
                               gcol=il * 128,
                               conv_out=(ncp[l, b] if i == 15 else None), gv_out=None, extra_caps=extra,
                               pre_caps=pre)
                S.begin_capture()
                stat_pool[0] = 1
                conv_taps(512)
                for il in range(4):
                    conv_post(128, il * 128, cat[:, il, :], "cat%d" % il)
                stat_pool[0] = 0
                cap_c = S.end_capture()
                S.begin_capture()
                prompt_attention(g)
                cap_a = S.end_capture()
                S.replay_spread([cap_a, cap_c])
            for il in range(4):
                i = 12 + il
                tile_back(l, 128, x_res[:, i, :], "x%d" % i, il, il * 128, cat[:, il, :], "cat%d" % il)
            S.barrier()
            mlp_phase(l, 128, [[(x_res[:, half * 8 + ti, :], "x%d" % (half * 8 + ti)) for ti in range(8)]
                               for half in range(2)])
            S.barrier()
        for i in range(16):
            sp_add(i_dma(yp[b, i * 128:(i + 1) * 128, :], x_res[:, i, :]), r=["x%d" % i])

    S.barrier()
    for s in range(NB if do_sample else 0):
        sp_add(i_dma(x_res[0:64, s, :], xs_d[s]), w=["x%d" % s])
    for l in range(depth if do_sample else 0):
        S.add("pool", i_ms(Vs.rearrange("p t (h e) -> p t h e", h=4)[:, :, :, 128:129], 1.0), (), ["Vs"])
        for bf in range(2):
            S.add("pool", i_ms(Vc[bf].rearrange("p t (h e) -> p t h e", h=4)[:, :, :, 128:129], 1.0), (), ["Vc%d" % bf])
        load_layer_consts(l, 64)
        load_w_in_out(l)
        for s in range(NB):
            conv_hist("dram", cc[l, s])
            front_N(64, x_res[0:64, s, :], "x%d" % s, s % 2)
            front_M(64, s % 2)
            tile_front(l, 64, x_res[0:64, s, :], "x%d" % s, s, s * 64, rope_s[0:64, :], "rope_s",
                       KTs[:, :, s * 64:(s + 1) * 64], "KTs",
                       Vs[0:64, s, :].rearrange("p (h e) -> p h e", h=4)[:, :, 0:128], "Vs",
                       cat[0:64, s, :], "cat%d" % s,
                       nks[l, s], nvs[l, s],
                       gcol=0,
                       conv_out=ncs[l, s], gv_out=ngs[l, s])
            conv_taps(64)
            conv_post(64, 0, cat[0:64, s, :], "cat%d" % s)
            sample_attention(l, s)
        for s in range(NB):
            tile_back(l, 64, x_res[0:64, s, :], "x%d" % s, s, s * 64, cat[0:64, s, :], "cat%d" % s)
        S.barrier()
        mlp_phase(l, 64, [[(x_res[0:64, s, :], "x%d" % s) for s in range(NB)]])
        S.barrier()
    for s in range(NB if do_sample else 0):
        sp_add(i_dma(ys[s], x_res[0:64, s, :]), r=["x%d" % s])

    S.finalize_and_emit()
    return nc


_NC_CACHE = {}


def _consts():
    ident = np.eye(128, dtype=np.float32)
    tril = np.tril(np.ones((128, 128), dtype=np.float32))
    freqs = (500000.0 ** (-np.arange(0, 16, 2, dtype=np.float32) / 16.0)).astype(np.float32)
    pos_p = np.arange(SEQ, dtype=np.float32)
    ang_p = (pos_p[:, None] * freqs[None, :]).astype(np.float32)
    rp = np.concatenate([np.cos(ang_p), np.sin(ang_p)], axis=1).astype(np.float32)
    rp = np.ascontiguousarray(rp.reshape(16, 128, 16).transpose(1, 0, 2))
    pos_s = (PAST + np.arange(DSEQ)).astype(np.float32)
    ang_s = (pos_s[:, None] * freqs[None, :]).astype(np.float32)
    rs = np.concatenate([np.cos(ang_s), np.sin(ang_s)], axis=1).astype(np.float32)
    return ident, tril, rp, np.ascontiguousarray(rs)


def kernel(x_prompt, x_sample, cache_k, cache_v, cache_conv, norm_mix_pre, norm_mix_post,
           norm_mlp_pre, norm_mlp_post, w_in, diff_lambda, diff_subln, gmlp_norm, gmlp_w_s,
           gmlp_bias, conv_w, conv_b, conv_ln_gain, conv_ln_bias, w_out, w_up, w_down):
    n = 8
    f = lambda a: np.ascontiguousarray(np.asarray(a, dtype=np.float32))
    if "nc" not in _NC_CACHE:
        _NC_CACHE["nc"] = build_nc()
    nc = _NC_CACHE["nc"]
    ident, tril, rp, rs = _consts()
    x_prompt = f(x_prompt)
    x_sample = f(x_sample)
    ckf = f(cache_k).reshape(DEPTH, 32, PAST, 512)
    cvf = f(cache_v).reshape(DEPTH, 32, PAST, 512)
    ccf = np.ascontiguousarray(f(cache_conv).reshape(DEPTH, 32, CW - 1, 2, 128).transpose(0, 1, 4, 3, 2))
    gpre = np.ascontiguousarray(np.stack([f(norm_mix_pre).reshape(DEPTH, 8, 128),
                                          f(norm_mlp_pre).reshape(DEPTH, 8, 128)], axis=1).transpose(0, 3, 1, 2))
    convw_t = np.ascontiguousarray(f(conv_w).reshape(DEPTH, CW, 2, 128).transpose(0, 3, 2, 1))
    convb_t = np.ascontiguousarray(f(conv_b).reshape(DEPTH, 2, 128).transpose(0, 2, 1))
    gbias_t = np.ascontiguousarray(f(gmlp_bias).transpose(0, 2, 1))
    shared = {
        "norm_pre_t": gpre, "norm_mix_post": f(norm_mix_post),
        "norm_mlp_post": f(norm_mlp_post),
        "w_in": f(w_in), "diff_lambda": f(diff_lambda).reshape(DEPTH, 256), "diff_subln": f(diff_subln),
        "gmlp_norm": f(gmlp_norm), "gmlp_w_s": f(gmlp_w_s), "gmlp_bias_t": gbias_t,
        "conv_w_t": convw_t, "conv_b_t": convb_t, "conv_ln_gain": f(conv_ln_gain),
        "conv_ln_bias": f(conv_ln_bias), "w_out": f(w_out), "w_up": f(w_up), "w_down": f(w_down),
        "c_ident": ident, "c_tril": tril, "c_rope_p": rp, "c_rope_s": rs,
    }
    in_maps = []
    for c in range(n):
        sl = slice(c * NB, (c + 1) * NB)
        m = dict(shared)
        m["x_prompt"] = np.ascontiguousarray(x_prompt[sl])
        m["x_sample"] = np.ascontiguousarray(x_sample[sl])
        m["cache_k"] = np.ascontiguousarray(ckf[:, sl])
        m["cache_v"] = np.ascontiguousarray(cvf[:, sl])
        m["cache_conv_t"] = np.ascontiguousarray(ccf[:, sl])
        in_maps.append(m)
    res = run_bass_kernel_spmd(nc, in_maps, core_ids=list(range(n)))
    R = res.results
    cat0 = lambda k: np.concatenate([r[k] for r in R], axis=0)
    cat1 = lambda k: np.concatenate([r[k] for r in R], axis=1)
    y_prompt = cat0("y_prompt")
    y_sample = cat0("y_sample")
    nkp = cat1("new_k_prompt").reshape(DEPTH, 32, SEQ, 8, 64)
    nvp = cat1("new_v_prompt").reshape(DEPTH, 32, SEQ, 4, 128)
    ncp = cat1("new_conv_prompt")
    nks = cat1("new_k_sample").reshape(DEPTH, 32, DSEQ, 8, 64)
    nvs = cat1("new_v_sample").reshape(DEPTH, 32, DSEQ, 4, 128)
    ncs = cat1("new_conv_sample")
    ngs = cat1("new_gmlp_v_sample").reshape(DEPTH, 32, DSEQ, 4, 64)
    return (y_prompt, y_sample, nkp, nvp, ncp, nks, nvs, ncs, ngs)
```

```python
import math
import contextlib
import numpy as np
import concourse.bass as bass
import concourse.mybir as mybir
from concourse.bass_utils import run_bass_kernel_spmd

F32 = mybir.dt.float32
BF16 = mybir.dt.bfloat16
AF = mybir.ActivationFunctionType
ALU = mybir.AluOpType
AX = mybir.AxisListType

D = 1024
DEPTH = 4
SEQ = 2048
NB = 4
DSEQ = 64
PAST = 4096
DIN = 2560
DFF = 4096
EPS = 1e-6
CW = 31


class Op:
    __slots__ = ("eng", "fn", "deps", "sig", "has_dep", "is_dma", "waits")

    def __init__(self, eng, fn, is_dma):
        self.eng = eng
        self.fn = fn
        self.deps = []
        self.sig = None
        self.has_dep = False
        self.is_dma = is_dma
        self.waits = []


class Sched:
    ENGS = ("pe", "act", "dve", "pool", "sp")
    NDSEM = 8
    GEN = 30000

    def __init__(self, nc):
        self.nc = nc
        self.ops = []
        self.per_eng = {e: [] for e in self.ENGS}
        self.last_w = {}
        self.readers = {}
        self.dma_count = {e: 0 for e in self.ENGS}
        self.dma_last = {}
        self.alias = {}
        self.expand = {}
        self._cap = None
        self._grp = None

    def begin_capture(self):
        self._cap = []
        self._grp = None

    def atomic_begin(self):
        if self._cap is not None:
            self._grp = []

    def atomic_end(self):
        if self._cap is not None and self._grp is not None:
            self._cap.append(self._grp)
            self._grp = None

    def end_capture(self):
        c = self._cap
        self._cap = None
        return c

    def replay_spread(self, lists):
        items = []
        for k, lst in enumerate(lists):
            n = len(lst)
            for i, it in enumerate(lst):
                items.append(((i + 0.5) / n, k, i, it))
        items.sort(key=lambda t: (t[0], t[1], t[2]))
        for _, _, _, item in items:
            if isinstance(item, list):
                for it in item:
                    self.add(*it)
            else:
                self.add(*item)

    def replay_rr(self, lists):
        idx = [0] * len(lists)
        left = sum(len(x) for x in lists)
        while left:
            for k, lst in enumerate(lists):
                if idx[k] < len(lst):
                    item = lst[idx[k]]
                    if isinstance(item, list):
                        for it in item:
                            self.add(*it)
                    else:
                        self.add(*item)
                    idx[k] += 1
                    left -= 1

    def barrier(self):
        lasts = [self.per_eng[e][-1] for e in self.ENGS if self.per_eng[e]]
        lasts += list(self.dma_last.values())
        for e in ("pe", "act", "dve", "pool", "sp"):
            self.add(e, lambda eng: eng.nop(), extra_deps=lasts)

    def add(self, eng, fn, reads=(), writes=(), dma=False, extra_deps=()):
        if self._cap is not None:
            rec = (eng, fn, list(reads), list(writes), dma, extra_deps)
            if self._grp is not None:
                self._grp.append(rec)
            else:
                self._cap.append(rec)
            return None
        op = Op(eng, fn, dma)
        deps = set(extra_deps)
        reads = [x for r in reads for x in self.expand.get(r, (self.alias.get(r, r),))]
        writes = [x for r in writes for x in self.expand.get(r, (self.alias.get(r, r),))]
        writes = writes + [r for r in reads if r.startswith("ps") and r not in writes]
        for r in reads:
            w = self.last_w.get(r)
            if w is not None:
                deps.add(w)
        for r in writes:
            w = self.last_w.get(r)
            if w is not None:
                deps.add(w)
            for rd in self.readers.get(r, ()):
                deps.add(rd)
        if dma:
            n = self.dma_count[eng]
            self.dma_count[eng] = n + 1
            j = n % self.NDSEM
            prev = self.dma_last.get((eng, j))
            if prev is not None:
                deps.add(prev)
            self.dma_last[(eng, j)] = op
            op.sig = (("d", eng, j), 16 * (n // self.NDSEM + 1))
        for d in deps:
            if d.eng == "pe" and eng == "pe" and not d.is_dma and not dma:
                continue
            d.has_dep = True
            op.deps.append(d)
        for r in reads:
            self.readers.setdefault(r, []).append(op)
        for r in writes:
            self.last_w[r] = op
            self.readers[r] = []
        self.ops.append(op)
        self.per_eng[eng].append(op)
        return op

    def finalize_and_emit(self):
        nc = self.nc
        cnt = {e: 0 for e in self.ENGS}
        G = self.GEN
        for op in self.ops:
            if op.is_dma:
                continue
            if op.has_dep:
                cnt[op.eng] += 1
                op.sig = (("e", op.eng), cnt[op.eng])
        clock = {e: {} for e in self.ENGS}
        for op in self.ops:
            ck = clock[op.eng]
            need = {}
            for d in op.deps:
                k, v = d.sig
                if ck.get(k, 0) >= v:
                    continue
                if need.get(k, 0) < v:
                    need[k] = v
            for k, v in need.items():
                ck[k] = v
                op.waits.append((k, v))
        keys = []
        for e in self.ENGS:
            for g in range((cnt[e] + G - 1) // G + 1):
                keys.append(("e", e, g))
        dma_final = {}
        for q in self.ENGS:
            n = self.dma_count[q]
            for j in range(min(self.NDSEM, n)):
                keys.append(("d", q, j))
                dma_final[("d", q, j)] = 16 * ((n - 1 - j) // self.NDSEM + 1)

        def semval(k, v):
            if k[0] == "d":
                return k, v
            return ("e", k[1], (v - 1) // G), (v - 1) % G + 1

        with contextlib.ExitStack() as st:
            sems = {}
            for k in keys:
                sems[k] = st.enter_context(nc.semaphore("s_" + "_".join(str(x) for x in k)))
            block = st.enter_context(nc.Block())

            def mk(ename):
                ops = self.per_eng[ename]

                def body(eng):
                    for op in ops:
                        for (k, v) in op.waits:
                            k2, v2 = semval(k, v)
                            eng.wait_ge(sems[k2], v2)
                        inst = op.fn(eng)
                        if op.is_dma:
                            inst.then_inc(sems[op.sig[0]], 16)
                        elif op.sig is not None:
                            k2, v2 = semval(*op.sig)
                            inst.then_inc(sems[k2], 1)
                    if ename == "sp":
                        for k, v in dma_final.items():
                            eng.wait_ge(sems[k], v)
                return body

            block.tensor(mk("pe"))
            block.scalar(mk("act"))
            block.vector(mk("dve"))
            block.gpsimd(mk("pool"))
            block.sync(mk("sp"))


def i_mm(out, lhsT, rhs, start=True, stop=True):
    return lambda e: e.matmul(out, lhsT, rhs, start=start, stop=stop)


def i_tr(out, in_, ident):
    return lambda e: e.transpose(out, in_, ident)


def i_act(out, in_, func, scale=None, bias=None, accum_out=None):
    kw = {}
    if scale is not None:
        kw["scale"] = scale
    if bias is not None:
        kw["bias"] = bias
    if accum_out is not None:
        kw["accum_out"] = accum_out
    return lambda e: e.activation(out, in_, func, **kw)


def i_ts(out, in0, s1, s2, op0, op1=None):
    if op1 is None:
        return lambda e: e.tensor_scalar(out, in0, s1, None, op0)
    return lambda e: e.tensor_scalar(out, in0, s1, s2, op0, op1)


def i_tt(out, in0, in1, op):
    return lambda e: e.tensor_tensor(out, in0, in1, op)


def i_stt(out, in0, scalar, in1, op0, op1):
    return lambda e: e.scalar_tensor_tensor(out, in0, scalar, in1, op0, op1)


def i_cp(out, in_):
    return lambda e: e.tensor_copy(out, in_)


def i_ms(ap, v):
    return lambda e: e.memset(ap, v)


def i_dma(out, in_, slow=False):
    if slow:
        return lambda e: e.dma_start(out=out, in_=in_, allow_slow_non_contiguous=True)
    return lambda e: e.dma_start(out=out, in_=in_)


def build_nc(n_prompt=NB, depth=DEPTH, do_sample=True):
    nc = bass.Bass("TRN2", target_bir_lowering=False, dynamic_dma_scratch_size=4096)
    S = Sched(nc)

    def din(name, shape):
        return nc.dram_tensor(name, list(shape), F32, kind="ExternalInput").ap()

    def dout(name, shape):
        return nc.dram_tensor(name, list(shape), F32, kind="ExternalOutput").ap()

    xp = din("x_prompt", [NB, SEQ, D])
    xs_d = din("x_sample", [NB, DSEQ, D])
    ck = din("cache_k", [DEPTH, NB, PAST, 512])
    cv = din("cache_v", [DEPTH, NB, PAST, 512])
    cc = din("cache_conv_t", [DEPTH, NB, 128, 2, CW - 1])
    gpre_t = din("norm_pre_t", [DEPTH, 128, 2, 8])
    g_mix_post = din("norm_mix_post", [DEPTH, D])
    g_mlp_post = din("norm_mlp_post", [DEPTH, D])
    w_in = din("w_in", [DEPTH, D, DIN])
    dlam = din("diff_lambda", [DEPTH, 256])
    dsub = din("diff_subln", [DEPTH, 128])
    gnorm = din("gmlp_norm", [DEPTH, 256])
    gws = din("gmlp_w_s", [DEPTH, 4, 128, 128])
    gbias = din("gmlp_bias_t", [DEPTH, 128, 4])
    conv_w = din("conv_w_t", [DEPTH, 128, 2, CW])
    conv_b = din("conv_b_t", [DEPTH, 128, 2])
    ln_g = din("conv_ln_gain", [DEPTH, 256])
    ln_b = din("conv_ln_bias", [DEPTH, 256])
    w_out = din("w_out", [DEPTH, D, D])
    w_up = din("w_up", [DEPTH, D, DFF])
    w_down = din("w_down", [DEPTH, DFF, D])
    c_ident = din("c_ident", [128, 128])
    c_tril = din("c_tril", [128, 128])
    c_rope_p = din("c_rope_p", [128, 16, 16])
    c_rope_s = din("c_rope_s", [64, 16])

    yp = dout("y_prompt", [NB, SEQ, D])
    ys = dout("y_sample", [NB, DSEQ, D])
    nkp = dout("new_k_prompt", [DEPTH, NB, SEQ, 512])
    nvp = dout("new_v_prompt", [DEPTH, NB, SEQ, 512])
    ncp = dout("new_conv_prompt", [DEPTH, NB, CW - 1, 256])
    nks = dout("new_k_sample", [DEPTH, NB, DSEQ, 512])
    nvs = dout("new_v_sample", [DEPTH, NB, DSEQ, 512])
    ncs = dout("new_conv_sample", [DEPTH, NB, CW - 1, 256])
    ngs = dout("new_gmlp_v_sample", [DEPTH, NB, DSEQ, 256])

    base0 = nc.sbuf_base
    top = nc.sbuf_top
    cur = [((base0 + 63) // 64) * 64]
    cnt_alloc = [0]

    def alloc(shape, dtype, at=None):
        nbytes = int(np.prod(shape[1:])) * (4 if dtype in (F32, mybir.dt.int32) else 2)
        nbytes = ((nbytes + 63) // 64) * 64
        if at is None:
            off = cur[0]
            cur[0] += nbytes
        else:
            off = at
        assert off + nbytes <= top, ("SBUF overflow", off, nbytes, top)
        cnt_alloc[0] += 1
        t = nc.alloc_sbuf_tensor_at("t%d" % cnt_alloc[0], list(shape), dtype, offset=off)
        return t.ap(), off, nbytes

    def A(shape, dtype):
        return alloc(shape, dtype)[0]

    x_off = cur[0]
    x_res = A([128, 16, D], F32)
    identf = A([128, 128], F32)
    identb = A([128, 128], BF16)
    tril = A([128, 128], F32)
    rope_p = A([128, 16, 16], F32)
    rope_s = A([128, 16], F32)
    gpreT = A([128, 2, 8], F32)
    subln_bc = A([128, 128], F32)
    gn_bc = A([128, 256], F32)
    lng_bc = A([128, 256], F32)
    lnb_bc = A([128, 256], F32)
    convwT = A([128, 2, CW], F32)
    convbT = A([128, 2], F32)
    gbiasT = A([128, 4], F32)
    wsT = A([128, 4, 128], BF16)
    sm = A([128, 64], F32)
    mhalf = A([128, 8], F32)
    stats = A([128, 256], F32)
    ov_base = cur[0]

    gpostA = A([128, D], F32)
    win = A([128, 8, DIN], BF16)
    wout = A([128, 8, D], BF16)
    kt_off = cur[0]
    KT = A([128, 4, SEQ], BF16)
    Vext = A([128, 16, 516], BF16)
    kv_end = cur[0]
    hTb = [A([128, 8, 128], BF16) for _ in range(2)]
    QT = A([128, 4, 512], BF16)
    catTb = [A([128, 8, 128], BF16)]
    cat_off = cur[0]
    cat = A([128, 4, D], BF16)
    gtmp_ws = [nc.alloc_sbuf_tensor_at("wsraw%d" % h, [128, 128], F32, offset=cat_off + h * 512).ap()
               for h in range(4)]
    for h in range(4):
        S.alias["wsraw%d" % h] = "cat0"
    lamq = nc.alloc_sbuf_tensor_at("lamq", [128, 256], F32, offset=cat_off + 2048).ap()
    S.alias["lamq"] = "cat1"
    xsbb = [A([128, D], BF16) for _ in range(2)]
    junk = A([128, 512], BF16)
    qr = A([128, 512], BF16)
    kf = A([128, 512], F32)
    kr = A([128, 512], BF16)
    vf = A([128, 512], F32)
    rtmpb = [A([128, 4, 64], F32) for _ in range(2)]
    vnb = A([128, 256], BF16)
    gluTg = A([128, 2, 30 + 512], F32)
    yTg = A([128, 2, 512], F32)
    ptmpP = A([128, 512], F32)
    Eb = [A([128, 512], BF16) for _ in range(2)]
    u0 = cur[0]
    Oe = A([128, 2, 4, 129], F32)
    at0 = A([128, 4, 128], F32)
    at1 = A([128, 4, 128], F32)
    ptmp = A([128, 512], F32)
    u1 = cur[0]
    cur[0] = u0
    gsq = A([128, 512], F32)
    gel = A([128, 512], F32)
    cur[0] = u0 + 4160
    vn = A([128, 256], F32)
    mtmp = A([128, 256], F32)
    glu = A([128, 256], F32)
    gtmp = A([128, 256], F32)
    cy = A([128, 256], F32)
    cy2 = A([128, 256], F32)
    assert cur[0] <= u1
    cur[0] = u1
    for nm in ("Oe0", "Oe1", "gsq", "gel"):
        S.alias[nm] = "U_a"
    for nm in ("at0", "vn", "mtmp"):
        S.alias[nm] = "U_b"
    for nm in ("at1", "glu", "gtmp"):
        S.alias[nm] = "U_c"
    for nm in ("ptmp", "cy", "cy2"):
        S.alias[nm] = "U_d"
    endA = cur[0]

    cur[0] = ov_base
    gpostB = A([128, D], F32)
    upT = A([128, 32, 1024], BF16)
    hT2_off = cur[0]
    hT2b = [A([128, 8, 1024], BF16) for _ in range(2)]
    f1 = A([128, 8, 512], F32)
    wupb = [A([128, 8, 512], BF16) for _ in range(2)]
    wdnb = [A([128, 4, 512], BF16) for _ in range(2)]
    rl = A([128, 512], BF16)
    rl_b = A([128, 512], BF16)
    xsb2 = A([128, D], BF16)
    junk2 = A([128, 512], BF16)
    ptmp2 = A([128, 512], F32)
    endB = cur[0]
    f0b = [nc.alloc_sbuf_tensor_at("f0alias%d" % k, [128, 8, 512], F32, offset=hT2_off + k * 16384).ap()
           for k in range(2)]

    cur[0] = kt_off
    KTs = A([128, 4, 256], BF16)
    Vs = A([128, 4, 516], BF16)
    Vc = [A([128, 8, 516], BF16) for _ in range(2)]
    assert cur[0] <= kv_end, (cur[0], kv_end)
    cur[0] = x_off + 4 * 4096
    kcb = [A([128, 8, 512], BF16) for _ in range(2)]
    KTc = [A([128, 4, 1024], BF16) for _ in range(2)]
    assert cur[0] <= x_off + 12 * 4096
    cur[0] = x_off + 12 * 4096
    Oacc = A([128, 8, 129], F32)
    cur[0] = max(endA, endB)
    print("SBUF used", cur[0], "of", top, "A", endA, "B", endB)

    ps_all = nc.alloc_psum_tensor("ps_all", [128, 8 * 512], F32).ap()
    ps = [ps_all[:, b * 512:(b + 1) * 512] for b in range(8)]
    psb = [p.bitcast(BF16) for p in ps]

    def PSR(b):
        return "ps%d" % b

    sp_add = lambda fn, r=(), w=(): S.add("sp", fn, r, w, dma=True)
    pool_dma = lambda fn, r=(), w=(): S.add("pool", fn, r, w, dma=True)

    sp_add(i_dma(identf, c_ident), w=["identf"])
    sp_add(i_dma(tril, c_tril), w=["tril"])
    sp_add(i_dma(rope_p, c_rope_p), w=["rope_p"])
    sp_add(i_dma(rope_s[0:64, :], c_rope_s), w=["rope_s"])
    S.add("dve", i_cp(identb, identf), ["identf"], ["identb"])
    S.add("pool", i_ms(mhalf, -0.5), (), ["mhalf"])

    stat_ctr = [0, 0]
    stat_pool = [0]

    def new_stat(n=1):
        p = stat_pool[0]
        k = p * 16 + stat_ctr[p] % 16
        stat_ctr[p] += 1
        return stats[:, k * 8:k * 8 + n], "stat%d" % k

    def rsqrt_mean(P, ssum_ap, ssum_res, n, width, scale_after=None):
        st, res = new_stat(width)
        st2, res2 = new_stat(width)
        S.add("pool", i_ts(st[:P], ssum_ap, 1.0 / n, EPS, ALU.mult, ALU.add), [ssum_res], [res])
        S.add("pool", i_tt(st2[:P], st[:P], mhalf[:P, 0:width], ALU.pow), [res, "mhalf"], [res2])
        return st2, res2

    def load_layer_consts(l, P):
        lam_init = 0.8 - 0.6 * math.exp(-0.3 * l)
        sp_add(i_dma(gpostA, g_mix_post[l].partition_broadcast(128), slow=True), w=["gpostA"])
        sp_add(i_dma(gpreT, gpre_t[l]), w=["gpreT"])
        sp_add(i_dma(subln_bc, dsub[l].partition_broadcast(128), slow=True), w=["subln"])
        sp_add(i_dma(gn_bc, gnorm[l].partition_broadcast(128), slow=True), w=["gn"])
        sp_add(i_dma(lng_bc, ln_g[l].partition_broadcast(128), slow=True), w=["lng"])
        sp_add(i_dma(lnb_bc, ln_b[l].partition_broadcast(128), slow=True), w=["lnb"])
        sp_add(i_dma(convwT, conv_w[l]), w=["convw"])
        sp_add(i_dma(convbT, conv_b[l]), w=["convb"])
        sp_add(i_dma(gbiasT, gbias[l]), w=["gbias"])
        sp_add(i_dma(lamq, dlam[l].partition_broadcast(128), slow=True), w=["lamq"])
        S.add("pool", i_ts(subln_bc, subln_bc, 1.0 - lam_init, 0.0, ALU.mult, ALU.add), ["subln"], ["subln"])
        S.add("dve", i_tt(lamq[:, 0:64], lamq[:, 0:64], lamq[:, 64:128], ALU.mult), ["lamq"], ["lamq"])
        S.add("dve", i_tt(lamq[:, 128:192], lamq[:, 128:192], lamq[:, 192:256], ALU.mult), ["lamq"], ["lamq"])
        S.add("dve", lambda e: e.tensor_reduce(sm[:, 0:1], lamq[:, 0:64], AX.X, ALU.add), ["lamq"], ["sm"])
        S.add("dve", lambda e: e.tensor_reduce(sm[:, 1:2], lamq[:, 128:192], AX.X, ALU.add), ["lamq"], ["sm"])
        S.add("act", i_act(sm[:, 2:4], sm[:, 0:2], AF.Exp), ["sm"], ["sm2"])
        S.add("dve", i_tt(sm[:, 4:5], sm[:, 3:4], sm[:, 2:3], ALU.subtract), ["sm2"], ["sm3"])
        S.add("dve", i_ts(sm[:, 5:6], sm[:, 4:5], -lam_init, None, ALU.add), ["sm3"], ["nlam"])
        for h in range(4):
            sp_add(i_dma(gtmp_ws[h], gws[l, h]), w=["wsraw%d" % h])
            S.add("dve", i_tt(gtmp_ws[h], gtmp_ws[h], tril, ALU.mult), ["wsraw%d" % h, "tril"], ["wsraw%d" % h])
            S.add("pe", i_tr(ps[7][:, h * 128:(h + 1) * 128], gtmp_ws[h], identf), ["wsraw%d" % h, "identf"], [PSR(7)])
        S.add("act", i_act(wsT.rearrange("p h t -> p (h t)"), ps[7][:, 0:512], AF.Copy), [PSR(7)], ["wsT"])

    def load_w_in_out(l):
        wi = w_in[l].rearrange("(c p) n -> p c n", p=128)
        for cg in range(5):
            pool_dma(i_dma(win[:, :, cg * 512:(cg + 1) * 512], wi[:, :, cg * 512:(cg + 1) * 512]),
                     w=["win%d" % cg])
        wo = w_out[l].rearrange("(c p) n -> p c n", p=128)
        for dh in range(2):
            pool_dma(i_dma(wout[:, :, dh * 512:(dh + 1) * 512], wo[:, :, dh * 512:(dh + 1) * 512]),
                     w=["wout%d" % dh])

    def norm_to_hT(P, xt, xres, which, xsb_, xsb_res, junk_, junk_res, dst, dst_res, dcol, tpbank):
        ss, ssr = new_stat(1)
        S.add("act", i_act(junk_[:P], xt, AF.Square, accum_out=ss[:P]), [xres], [junk_res, ssr])
        rstd, rr = rsqrt_mean(P, ss[:P], ssr, D, 1)
        S.add("dve", i_ts(xsb_[:P], xt, rstd[:P], None, ALU.mult), [xres, rr], [xsb_res])
        S.atomic_begin()
        for c in range(8):
            S.add("pe", i_tr(psb[tpbank][:, c * P:(c + 1) * P], xsb_[:P, c * 128:(c + 1) * 128], identb[:P, :P]),
                  [xsb_res, "identb"], ["ps0"])
        S.add("dve", i_tt(dst[:, :, dcol:dcol + P],
                          psb[tpbank][:, 0:8 * P].rearrange("p (c t) -> p c t", c=8),
                          gpreT[:, which, :].unsqueeze(2).to_broadcast([128, 8, P]), ALU.mult),
              ["ps0", "gpreT"], [dst_res])
        S.atomic_end()

    def post_norm_residual(P, xt, xres, which, banks, ptmp_, ptmp_res, junk_, junk_res, srcs=None):
        if srcs is None:
            srcs = [(ps[banks[0]][:P, :], PSR(banks[0])), (ps[banks[1]][:P, :], PSR(banks[1]))]
        ss, ssr = new_stat(2)
        for dh in range(2):
            S.add("act", i_act(junk_[:P, 0:512], srcs[dh][0], AF.Square, accum_out=ss[:P, dh:dh + 1]),
                  [srcs[dh][1]], [junk_res, ssr])
        st, sr = new_stat(1)
        S.add("pool", i_tt(st[:P], ss[:P, 0:1], ss[:P, 1:2], ALU.add), [ssr], [sr])
        rstd, rr = rsqrt_mean(P, st[:P], sr, D, 1)
        for dh in range(2):
            gp = gpostA if which == 0 else gpostB
            gpr = "gpostA" if which == 0 else "gpostB"
            S.add("dve", i_stt(ptmp_[:P], srcs[dh][0], rstd[:P], gp[:P, dh * 512:(dh + 1) * 512],
                               ALU.mult, ALU.mult), [srcs[dh][1], rr, gpr], [ptmp_res])
            S.add("pool", i_tt(xt[:, dh * 512:(dh + 1) * 512], xt[:, dh * 512:(dh + 1) * 512], ptmp_[:P], ALU.add),
                  [xres, ptmp_res], [xres])

    def rope(P, zb_, zres, cs, cs_res, dstf, dst_res, rk=0):
        z3 = zb_.rearrange("p (h d) -> p h d", h=8)
        d3 = dstf.rearrange("p (h d) -> p h d", h=8)
        cosb = cs[:, 0:8].unsqueeze(1).to_broadcast([P, 8, 8])
        sinb = cs[:, 8:16].unsqueeze(1).to_broadcast([P, 8, 8])
        rtmp = rtmpb[rk]
        t = [rtmp[:P, k, :].rearrange("p (h d) -> p h d", h=8) for k in range(4)]
        x1 = z3[:, :, 0:8]
        x2 = z3[:, :, 8:16]
        zr = [zres]
        S.add("dve", i_tt(t[0], x1, cosb, ALU.mult), zr + [cs_res], ["rt0_%d" % rk])
        S.add("dve", i_tt(t[1], x2, sinb, ALU.mult), zr + [cs_res], ["rt1_%d" % rk])
        S.add("dve", i_tt(t[2], x2, cosb, ALU.mult), zr + [cs_res], ["rt2_%d" % rk])
        S.add("dve", i_tt(t[3], x1, sinb, ALU.mult), zr + [cs_res], ["rt3_%d" % rk])
        S.add("dve", i_tt(d3[:, :, 0:8], t[0], t[1], ALU.subtract), ["rt0_%d" % rk, "rt1_%d" % rk], [dst_res])
        S.add("dve", i_tt(d3[:, :, 8:16], t[2], t[3], ALU.add), ["rt2_%d" % rk, "rt3_%d" % rk], [dst_res])
        S.add("act", i_act(d3[:, :, 16:64], z3[:, :, 16:64], AF.Copy), zr, [dst_res])

    def diff_combine(P, O0, O1, ores, out4, out_res):
        rr, rres = new_stat(8)
        S.add("dve", lambda e: e.reciprocal(rr[:P, 0:4], O0[:, :, 128]), ores, [rres])
        S.add("dve", lambda e: e.reciprocal(rr[:P, 4:8], O1[:, :, 128]), ores, [rres])
        S.add("dve", i_ts(rr[:P, 4:8], rr[:P, 4:8], sm[:P, 5:6], None, ALU.mult), [rres, "nlam"], [rres])
        S.add("dve", i_tt(at0[:P], O0[:, :, 0:128], rr[:P, 0:4].unsqueeze(2).to_broadcast([P, 4, 128]), ALU.mult),
              ores + [rres], ["at0"])
        S.add("dve", i_tt(at1[:P], O1[:, :, 0:128], rr[:P, 4:8].unsqueeze(2).to_broadcast([P, 4, 128]), ALU.mult),
              ores + [rres], ["at1"])
        S.add("dve", i_tt(at0[:P], at0[:P], at1[:P], ALU.add), ["at0", "at1"], ["at0"])
        S.add("dve", i_tt(at1[:P], at0[:P], at0[:P], ALU.mult), ["at0"], ["at1"])
        ss, ssr = new_stat(4)
        S.add("dve", lambda e: e.tensor_reduce(ss[:P], at1[:P], AX.X, ALU.add), ["at1"], [ssr])
        rstd, rsr = rsqrt_mean(P, ss[:P], ssr, 128, 4)
        S.add("dve", i_tt(at0[:P], at0[:P], rstd[:P].unsqueeze(2).to_broadcast([P, 4, 128]), ALU.mult),
              ["at0", rsr], ["at0"])
        S.add("dve", i_tt(out4, at0[:P], subln_bc[:P].unsqueeze(1).to_broadcast([P, 4, 128]), ALU.mult),
              ["at0", "subln"], list(out_res))

    def front_N(P, xt, xres, par):
        norm_to_hT(P, xt, xres, 0, xsbb[par], "xsb%d" % par, xsbb[par], "xsb%d" % par,
                   hTb[par], "hT%d" % par, 0, 0)

    def front_M(P, par):
        hT = hTb[par]
        for cg in range(5):
            for c in range(8):
                S.add("pe", i_mm(ps[1 + cg][:P, :], hT[:, c, 0:P], win[:, c, cg * 512:(cg + 1) * 512],
                                 start=(c == 0), stop=(c == 7)), ["hT%d" % par, "win%d" % cg], [PSR(1 + cg)])

    def tile_front(l, P, xt, xres, il, hcol, cs, cs_res, kt_dst, kt_res, v_dst, v_res, cat_t, cat_res,
                   k_out, v_out, gcol, conv_out, gv_out, extra_caps=(), pre_caps=()):
        caps = list(pre_caps)
        S.begin_capture()
        rope(P, ps[1][:P, :], PSR(1), cs, cs_res, qr[:P], "qr", rk=0)
        S.atomic_begin()
        for j in range(4):
            S.add("pe", i_tr(psb[0][:, j * P:(j + 1) * P], qr[:P, j * 128:(j + 1) * 128], identb[:P, :P]),
                  ["qr", "identb"], ["ps0"])
        S.add("dve", i_cp(QT[:, :, hcol:hcol + P], psb[0][:, 0:4 * P].rearrange("p (c t) -> p c t", c=4)),
              ["ps0"], ["QT"])
        S.atomic_end()
        caps.append(S.end_capture())
        S.begin_capture()
        rope(P, ps[2][:P, :], PSR(2), cs, cs_res, kf[:P], "kf", rk=1)
        sp_add(i_dma(k_out, kf[:P]), r=["kf"])
        S.add("pool", i_cp(kr[:P], kf[:P]), ["kf"], ["kr"])
        S.atomic_begin()
        for j in range(4):
            S.add("pe", i_tr(psb[0][:, 512 + j * P:512 + (j + 1) * P], kr[:P, j * 128:(j + 1) * 128], identb[:P, :P]),
                  ["kr", "identb"], ["ps0"])
        S.add("dve", i_cp(kt_dst, psb[0][:, 512:512 + 4 * P].rearrange("p (c t) -> p c t", c=4)), ["ps0"], [kt_res])
        S.atomic_end()
        caps.append(S.end_capture())
        S.begin_capture()
        S.add("act", i_act(vf[:P], ps[3][:P, :], AF.Copy), [PSR(3)], ["vf"])
        sp_add(i_dma(v_out, vf[:P]), r=["vf"])
        S.add("pool", i_cp(v_dst, vf[:P].rearrange("p (h e) -> p h e", h=4)), ["vf"], [v_res])
        caps.append(S.end_capture())
        S.begin_capture()
        zb = ps[4][:P, :]
        S.add("act", i_act(gel[:P], zb, AF.Copy), [PSR(4)], ["gel"])
        S.add("act", i_act(gsq[:P], gel[:P], AF.Square), ["gel"], ["gsq"])
        S.add("dve", i_ts(gsq[:P], gsq[:P], 0.044715, 1.0, ALU.mult, ALU.add), ["gsq"], ["gsq"])
        S.add("dve", i_tt(gsq[:P], gsq[:P], gel[:P], ALU.mult), ["gsq", "gel"], ["gsq"])
        S.add("act", i_act(gsq[:P], gsq[:P], AF.Tanh, scale=0.7978845608028654), ["gsq"], ["gsq"])
        S.add("dve", i_ts(gsq[:P], gsq[:P], 0.5, 0.5, ALU.mult, ALU.add), ["gsq"], ["gsq"])
        S.add("dve", i_tt(gel[:P], gsq[:P], gel[:P], ALU.mult), ["gsq", "gel"], ["gel"])
        ss, ssr = new_stat(1)
        S.add("act", i_act(mtmp[:P], gel[:P, 256:512], AF.Square, accum_out=ss[:P]), ["gel"], ["mtmp", ssr])
        rstd, rr = rsqrt_mean(P, ss[:P], ssr, 256, 1)
        S.add("dve", i_stt(vn[:P], gel[:P, 256:512], rstd[:P], gn_bc[:P], ALU.mult, ALU.mult),
              ["gel", rr, "gn"], ["vn"])
        if gv_out is not None:
            sp_add(i_dma(gv_out, vn[:P]), r=["vn"])
        S.add("act", i_act(vnb[:P], vn[:P], AF.Copy), ["vn"], ["vnb"])
        S.atomic_begin()
        for h in range(4):
            S.add("pe", i_mm(ps[6][:P, h * 64:(h + 1) * 64], wsT[:P, h, :P], vnb[:P, h * 64:(h + 1) * 64]),
                  ["wsT", "vnb"], [PSR(6)])
        S.add("dve", i_tt(mtmp[:P].rearrange("p (h d) -> p h d", h=4),
                          ps[6][:P, 0:256].rearrange("p (h d) -> p h d", h=4),
                          gbiasT[:P].unsqueeze(2).to_broadcast([P, 4, 64]), ALU.add),
              [PSR(6), "gbias"], ["mtmp"])
        S.atomic_end()
        S.add("dve", i_tt(cat_t[:, 512:768], mtmp[:P], gel[:P, 0:256], ALU.mult), ["mtmp", "gel"], [cat_res])
        caps.append(S.end_capture())
        S.begin_capture()
        zc = ps[5][:P, :]
        S.add("act", i_act(gtmp[:P], zc[:, 256:512], AF.Tanh, scale=0.5), [PSR(5)], ["gtmp"])
        S.add("act", i_act(glu[:P], zc[:, 0:256], AF.Copy), [PSR(5)], ["glu"])
        S.add("dve", i_ts(gtmp[:P], gtmp[:P], 0.5, 0.5, ALU.mult, ALU.add), ["gtmp"], ["gtmp"])
        S.add("dve", i_tt(glu[:P], gtmp[:P], glu[:P], ALU.mult), ["gtmp", "glu"], ["glu"])
        if conv_out is not None:
            sp_add(i_dma(conv_out, glu[P - 30:P, :]), r=["glu"])
        S.atomic_begin()
        for c2 in range(2):
            S.add("pe", i_tr(ps[7][:, c2 * P:(c2 + 1) * P], glu[:P, c2 * 128:(c2 + 1) * 128], identf[:P, :P]),
                  ["glu", "identf"], [PSR(7)])
        S.add("act", i_act(gluTg[:, :, 30 + gcol:30 + gcol + P], ps[7][:, 0:2 * P].rearrange("p (c t) -> p c t", c=2),
                           AF.Copy), [PSR(7)], ["gluTg"])
        S.atomic_end()
        caps.append(S.end_capture())
        caps.extend(extra_caps)
        S.replay_rr(caps)

    def conv_hist(mode, src=None, W=512):
        if mode == "zero":
            S.add("pool", i_ms(gluTg[:, :, 0:30], 0.0), (), ["gluTg"])
        elif mode == "carry":
            S.add("pool", i_cp(gluTg[:, :, 0:30], gluTg[:, :, W:W + 30]), ["gluTg"], ["gluTg"])
        else:
            sp_add(i_dma(gluTg[:, :, 0:30], src), w=["gluTg"])

    def conv_taps(W):
        S.add("dve", i_ts(yTg[:, 0, 0:W], gluTg[:, 0, 0:W], convwT[:, 0, 0:1], convbT[:, 0:1], ALU.mult, ALU.add),
              ["gluTg", "convw", "convb"], ["yT0"])
        S.add("pool", i_ts(yTg[:, 1, 0:W], gluTg[:, 1, 0:W], convwT[:, 1, 0:1], convbT[:, 1:2], ALU.mult, ALU.add),
              ["gluTg", "convw", "convb"], ["yT1"])
        for j in range(1, CW):
            S.add("dve", i_stt(yTg[:, 0, 0:W], gluTg[:, 0, j:j + W], convwT[:, 0, j:j + 1], yTg[:, 0, 0:W],
                               ALU.mult, ALU.add), ["gluTg", "convw", "yT0"], ["yT0"])
            S.add("act", i_act(ptmpP[:, 0:W], gluTg[:, 1, j:j + W], AF.Copy, scale=convwT[:, 1, j:j + 1]),
                  ["gluTg", "convw"], ["ptmpP"])
            S.add("pool", i_tt(yTg[:, 1, 0:W], yTg[:, 1, 0:W], ptmpP[:, 0:W], ALU.add), ["ptmpP", "yT1"], ["yT1"])

    def conv_post(P, gcol, cat_t, cat_res):
        S.atomic_begin()
        for c2 in range(2):
            S.add("pe", i_tr(ps[7][:P, c2 * 128:(c2 + 1) * 128], yTg[:, c2, gcol:gcol + P], identf),
                  ["yT%d" % c2, "identf"], [PSR(7)])
        st6, s6r = new_stat(6)
        mv, mvr = new_stat(2)
        S.add("dve", lambda e: e.bn_stats(st6[:P], ps[7][:P, 0:256]), [PSR(7)], [s6r])
        S.add("dve", lambda e: e.bn_aggr(mv[:P], st6[:P]), [s6r], [mvr])
        rstd, rr = rsqrt_mean(P, mv[:P, 1:2], mvr, 1, 1)
        S.add("dve", i_ts(cy[:P], ps[7][:P, 0:256], mv[:P, 0:1], rstd[:P], ALU.subtract, ALU.mult),
              [PSR(7), mvr, rr], ["cy"])
        S.atomic_end()
        S.add("dve", i_tt(cy[:P], cy[:P], lng_bc[:P], ALU.mult), ["cy", "lng"], ["cy"])
        S.add("dve", i_tt(cy[:P], cy[:P], lnb_bc[:P], ALU.add), ["cy", "lnb"], ["cy"])
        S.add("act", i_act(cy2[:P], cy[:P], AF.Tanh, scale=0.5), ["cy"], ["cy2"])
        S.add("dve", i_ts(cy2[:P], cy2[:P], 0.5, 0.5, ALU.mult, ALU.add), ["cy2"], ["cy2"])
        S.add("dve", i_tt(cat_t[:, 768:1024], cy2[:P], cy[:P], ALU.mult), ["cy2", "cy"], [cat_res])

    def tile_back(l, P, xt, xres, il, hcol, cat_t, cat_res, par=0):
        cT = catTb[par]
        cres = "catT%d" % par
        S.atomic_begin()
        for c in range(8):
            S.add("pe", i_tr(psb[0][:, c * P:(c + 1) * P], cat_t[:, c * 128:(c + 1) * 128], identb[:P, :P]),
                  [cat_res, "identb"], ["ps0"])
        S.add("act", i_act(cT[:, :, 0:P], psb[0][:, 0:8 * P].rearrange("p (c t) -> p c t", c=8), AF.Copy),
              ["ps0"], [cres])
        for dh in range(2):
            for c in range(8):
                S.add("pe", i_mm(ps[6 + dh][:P, :], cT[:, c, 0:P], wout[:, c, dh * 512:(dh + 1) * 512],
                                 start=(c == 0), stop=(c == 7)), [cres, "wout%d" % dh], [PSR(6 + dh)])
        post_norm_residual(P, xt, xres, 0, (6, 7), ptmp, "ptmp", junk, "junk")
        S.atomic_end()

    def mlp_norm(l, P, tiles, hb):
        for ti, (xt, xres) in enumerate(tiles):
            norm_to_hT(P, xt, xres, 1, xsb2, "xsb2", xsb2, "xsb2", hT2b[hb], "hT2_%d" % hb, ti * P, 0)

    up_rot = [0]

    def mlp_up(l, P, n, hb):
        W = n * P
        hT2 = hT2b[hb]
        wu = w_up[l].rearrange("(c p) n -> p c n", p=128)
        nsub = (W + 511) // 512

        def load_up(cg):
            pool_dma(i_dma(wupb[cg % 2], wu[:, :, cg * 512:(cg + 1) * 512]), w=["wup%d" % (cg % 2)])

        load_up(0)
        for cg in range(8):
            if cg + 1 < 8:
                load_up(cg + 1)
            for jj in range(4):
                j = cg * 4 + jj
                for sb in range(nsub):
                    c0 = sb * 512
                    wdt = min(512, W - c0)
                    b = 1 + (up_rot[0] % 4)
                    up_rot[0] += 1
                    for c in range(8):
                        S.add("pe", i_mm(ps[b][:, 0:wdt], wupb[cg % 2][:, c, jj * 128:(jj + 1) * 128],
                                         hT2[:, c, c0:c0 + wdt], start=(c == 0), stop=(c == 7)),
                              ["wup%d" % (cg % 2), "hT2_%d" % hb], [PSR(b)])
                    rb = "rl%d" % (up_rot[0] % 2)
                    rlt = rl if (up_rot[0] % 2) else rl_b
                    S.add("act", i_act(rlt[:, 0:wdt], ps[b][:, 0:wdt], AF.Relu), [PSR(b)], [rb])
                    S.add("dve", i_tt(upT[:, j, c0:c0 + wdt], rlt[:, 0:wdt], rlt[:, 0:wdt], ALU.mult), [rb], ["upT"])

    def mlp_down(l, P, n, hb):
        wd = w_down[l].rearrange("(j p) d -> p j d", p=128)

        def load_dn(k):
            dh, jq = divmod(k, 8)
            pool_dma(i_dma(wdnb[k % 2], wd[:, jq * 4:(jq + 1) * 4, dh * 512:(dh + 1) * 512]), w=["wdn%d" % (k % 2)])

        load_dn(0)
        for dh in range(2):
            for jq in range(8):
                k = dh * 8 + jq
                if k + 1 < 16:
                    load_dn(k + 1)
                for jj in range(4):
                    j = jq * 4 + jj
                    for ti in range(n):
                        S.add("pe", i_mm(ps[ti][:P, :], upT[:, j, ti * P:(ti + 1) * P], wdnb[k % 2][:, jj, :],
                                         start=(j == 0), stop=(j == 31)), ["upT", "wdn%d" % (k % 2)], [PSR(ti)])
            for ti in range(n):
                if dh == 0:
                    S.add("act", i_act(f0b[hb][:P, ti, :], ps[ti][:P, :], AF.Copy), [PSR(ti)], ["hT2_%d" % hb])
                elif ti % 2 == 0:
                    S.add("act", i_act(f1[:P, ti, :], ps[ti][:P, :], AF.Copy), [PSR(ti)], ["f1"])
                else:
                    S.add("dve", i_cp(f1[:P, ti, :], ps[ti][:P, :]), [PSR(ti)], ["f1"])

    def mlp_post(l, P, tiles, hb):
        for ti, (xt, xres) in enumerate(tiles):
            post_norm_residual(P, xt, xres, 1, None, ptmp2, "ptmp2", junk2, "junk2",
                               srcs=[(f0b[hb][:P, ti, :], "hT2_%d" % hb), (f1[:P, ti, :], "f1")])

    def mlp_phase(l, P, groups):
        sp_add(i_dma(gpostB, g_mlp_post[l].partition_broadcast(128), slow=True), w=["gpostB"])
        mlp_norm(l, P, groups[0], 0)
        for k, tiles in enumerate(groups):
            hb = k % 2
            caps = []
            S.begin_capture()
            mlp_up(l, P, len(tiles), hb)
            caps.append(S.end_capture())
            if k + 1 < len(groups):
                S.begin_capture()
                mlp_norm(l, P, groups[k + 1], 1 - hb)
                caps.append(S.end_capture())
            if k > 0:
                S.begin_capture()
                stat_pool[0] = 1
                mlp_post(l, P, groups[k - 1], 1 - hb)
                stat_pool[0] = 0
                caps.append(S.end_capture())
            S.replay_rr(caps)
            mlp_down(l, P, len(tiles), hb)
        mlp_post(l, P, groups[-1], (len(groups) - 1) % 2)

    def prompt_attention(g):
        its = []
        for h in range(4):
            for m in range(2):
                for j in range(4 * g + 4):
                    its.append((h, m, j))

        def emit_S(k):
            h, m, j = its[k]
            hs = 2 * h + m
            r0 = (hs % 2) * 64
            n0 = max(0, j - 4 * g) * 128
            sb = 1 + (k % 2)
            S.add("pe", i_mm(ps[sb][:, n0:512], KT[r0:r0 + 64, h, j * 128:(j + 1) * 128],
                             QT[r0:r0 + 64, h, n0:512]), ["KT", "QT"], [PSR(sb)])

        emit_S(0)
        for k, (h, m, j) in enumerate(its):
            il0 = max(0, j - 4 * g)
            n0 = il0 * 128
            sb = 1 + (k % 2)
            eb = k % 2
            S.add("act", i_act(Eb[eb][:, n0:512], ps[sb][:, n0:512], AF.Exp, scale=0.125),
                  [PSR(sb)], ["E%d" % eb])
            if j >= 4 * g:
                S.add("act", i_act(Eb[eb][64:128, n0:n0 + 64], Eb[eb][64:128, n0:n0 + 64], AF.Copy, scale=0.0),
                      ["E%d" % eb], ["E%d" % eb])
            if k + 1 < len(its):
                emit_S(k + 1)
            for il in range(il0, 4):
                S.add("pe", i_mm(ps[3 + il][:, 0:129], Eb[eb][:, il * 128:(il + 1) * 128],
                                 Vext[:, j, h * 129:(h + 1) * 129], start=(j == 0), stop=(j == 4 * g + il)),
                      ["E%d" % eb, "Vext"], [PSR(3 + il)])
            if j == 4 * g + 3:
                S.add("dve", i_cp(Oe[:, m], ps_all[:, 3 * 512:7 * 512].rearrange("p (b n) -> p b n", b=4)[:, :, 0:129]),
                      [PSR(3), PSR(4), PSR(5), PSR(6)], ["Oe%d" % m])
                if m == 1:
                    diff_combine(128, Oe[:, 0], Oe[:, 1], ["Oe0", "Oe1"], cat[:, :, h * 128:(h + 1) * 128],
                                     ["cat0", "cat1", "cat2", "cat3"])

    def sample_attention(l, s):
        P = 64
        ckl = ck[l, s].rearrange("(kt p) f -> p kt f", p=128)
        cvl = cv[l, s].rearrange("(kt p) f -> p kt f", p=128)

        def load_chunk(ch):
            bf = ch % 2
            pool_dma(i_dma(kcb[bf], ckl[:, ch * 8:(ch + 1) * 8, :]), w=["kcb%d" % bf])
            for h4 in range(4):
                pool_dma(i_dma(Vc[bf][:, :, h4 * 129:h4 * 129 + 128],
                               cvl[:, ch * 8:(ch + 1) * 8, h4 * 128:(h4 + 1) * 128]),
                         w=["Vc%d" % bf])

        load_chunk(0)
        sbank = 0
        for ch in range(5):
            bf = ch % 2
            if ch + 1 < 4:
                load_chunk(ch + 1)
            if ch < 4:
                for kp in range(4):
                    tb = 0 if kp % 2 == 0 else 7
                    for hp in range(4):
                        for kl in range(2):
                            kt = kp * 2 + kl
                            S.add("pe", i_tr(psb[tb][:, (hp * 2 + kl) * 128:(hp * 2 + kl + 1) * 128],
                                             kcb[bf][:, kt, hp * 128:(hp + 1) * 128], identb),
                                  ["kcb%d" % bf, "identb"], [PSR(tb)])
                    S.add("dve", i_cp(KTc[bf][:, :, kp * 256:(kp + 1) * 256],
                                      psb[tb][:, 0:1024].rearrange("p (c t) -> p c t", c=4)), [PSR(tb)], ["KTc%d" % bf])
            for hs in range(8):
                r0 = (hs % 2) * 64
                hp = hs // 2
                sb = 1 + (sbank % 2)
                eb = sbank % 2
                sbank += 1
                ob = 3 + hs // 3
                oc = (hs % 3) * 160
                if ch < 4:
                    for kt in range(8):
                        S.add("pe", i_mm(ps[sb][:, kt * 64:(kt + 1) * 64], KTc[bf][r0:r0 + 64, hp, kt * 128:(kt + 1) * 128],
                                         QT[r0:r0 + 64, hp, s * 64:(s + 1) * 64]), ["KTc%d" % bf, "QT"], [PSR(sb)])
                    S.add("act", i_act(Eb[eb][:, :], ps[sb][:, :], AF.Exp, scale=0.125), [PSR(sb)], ["E%d" % eb])
                    for kt in range(8):
                        S.add("pe", i_mm(ps[ob][:64, oc:oc + 129], Eb[eb][:, kt * 64:(kt + 1) * 64],
                                         Vc[bf][:, kt, hp * 129:(hp + 1) * 129], start=(kt == 0), stop=(kt == 7)),
                              ["E%d" % eb, "Vc%d" % bf], [PSR(ob)])
                else:
                    S.add("pe", i_mm(ps[sb][:64, 0:64], KTs[r0:r0 + 64, hp, s * 64:(s + 1) * 64],
                                     QT[r0:r0 + 64, hp, s * 64:(s + 1) * 64]), ["KTs", "QT"], [PSR(sb)])
                    S.add("act", i_act(Eb[eb][:64, 0:64], ps[sb][:64, 0:64], AF.Exp, scale=0.125), [PSR(sb)], ["E%d" % eb])
                    S.add("pe", i_mm(ps[ob][:64, oc:oc + 129], Eb[eb][:64, 0:64],
                                     Vs[:64, s, hp * 129:(hp + 1) * 129]), ["E%d" % eb, "Vs"], [PSR(ob)])
            for b3 in range(3):
                nh = 3 if b3 < 2 else 2
                src = ps[3 + b3][:64, 0:480].rearrange("p (a n) -> p a n", a=3)[:, 0:nh, 0:129]
                dst = Oacc[:64, 3 * b3:3 * b3 + nh, :]
                if ch == 0:
                    S.add("dve", i_cp(dst, src), [PSR(3 + b3)], ["Oacc"])
                else:
                    S.add("dve", i_tt(dst, dst, src, ALU.add), [PSR(3 + b3), "Oacc"], ["Oacc"])
        O4 = Oacc[:64].rearrange("p (h m) n -> p h m n", m=2)
        diff_combine(64, O4[:, :, 0, :], O4[:, :, 1, :], ["Oacc"],
                     cat[:64, s, 0:512].rearrange("p (h e) -> p h e", h=4), ["cat%d" % s])

    for b in range(n_prompt):
        for i in range(16):
            sp_add(i_dma(x_res[:, i, :], xp[b, i * 128:(i + 1) * 128, :]), w=["x%d" % i])
        for l in range(depth):
            S.add("pool", i_ms(Vext.rearrange("p t (h e) -> p t h e", h=4)[:, :, :, 128:129], 1.0), (), ["Vext"])
            load_layer_consts(l, 128)
            load_w_in_out(l)
            front_N(128, x_res[:, 0, :], "x0", 0)
            for g in range(4):
                conv_hist("zero" if g == 0 else "carry")
                for il in range(4):
                    i = 4 * g + il
                    front_M(128, i % 2)
                    pre = []
                    if g > 0:
                        ib = 4 * (g - 1) + il
                        S.begin_capture()
                        tile_back(l, 128, x_res[:, ib, :], "x%d" % ib, il, il * 128, cat[:, il, :], "cat%d" % il)
                        pre.append(S.end_capture())
                    extra = []
                    if i + 1 < 16:
                        S.begin_capture()
                        front_N(128, x_res[:, i + 1, :], "x%d" % (i + 1), (i + 1) % 2)
                        extra.append(S.end_capture())
                    tile_front(l, 128, x_res[:, i, :], "x%d" % i, il, il * 128, rope_p[:, i, :], "rope_p",
                               KT[:, :, i * 128:(i + 1) * 128], "KT",
                               Vext[:, i, :].rearrange("p (h e) -> p h e", h=4)[:, :, 0:128], "Vext",
                               cat[:, il, :], "cat%d" % il,
                               nkp[l, b, i * 128:(i + 1) * 128, :], nvp[l, b, i * 128:(i + 1) * 128, :],
                               gcol=il * 128,
                               conv_out=(ncp[l, b] if i == 15 else None), gv_out=None, extra_caps=extra,
                               pre_caps=pre)
                S.begin_capture()
                stat_pool[0] = 1
                conv_taps(512)
                for il in range(4):
                    conv_post(128, il * 128, cat[:, il, :], "cat%d" % il)
                stat_pool[0] = 0
                cap_c = S.end_capture()
                S.begin_capture()
                prompt_attention(g)
                cap_a = S.end_capture()
                S.replay_spread([cap_a, cap_c])
            for il in range(4):
                i = 12 + il
                tile_back(l, 128, x_res[:, i, :], "x%d" % i, il, il * 128, cat[:, il, :], "cat%d" % il)
            S.barrier()
            mlp_phase(l, 128, [[(x_res[:, half * 8 + ti, :], "x%d" % (half * 8 + ti)) for ti in range(8)]
                               for half in range(2)])
            S.barrier()
        for i in range(16):
            sp_add(i_dma(yp[b, i * 128:(i + 1) * 128, :], x_res[:, i, :]), r=["x%d" % i])

    S.barrier()
    for s in range(NB if do_sample else 0):
        sp_add(i_dma(x_res[0:64, s, :], xs_d[s]), w=["x%d" % s])
    for l in range(depth if do_sample else 0):
        S.add("pool", i_ms(Vs.rearrange("p t (h e) -> p t h e", h=4)[:, :, :, 128:129], 1.0), (), ["Vs"])
        for bf in range(2):
            S.add("pool", i_ms(Vc[bf].rearrange("p t (h e) -> p t h e", h=4)[:, :, :, 128:129], 1.0), (), ["Vc%d" % bf])
        load_layer_consts(l, 64)
        load_w_in_out(l)
        for s in range(NB):
            conv_hist("dram", cc[l, s])
            front_N(64, x_res[0:64, s, :], "x%d" % s, s % 2)
            front_M(64, s % 2)
            tile_front(l, 64, x_res[0:64, s, :], "x%d" % s, s, s * 64, rope_s[0:64, :], "rope_s",
                       KTs[:, :, s * 64:(s + 1) * 64], "KTs",
                       Vs[0:64, s, :].rearrange("p (h e) -> p h e", h=4)[:, :, 0:128], "Vs",
                       cat[0:64, s, :], "cat%d" % s,
                       nks[l, s], nvs[l, s],
                       gcol=0,
                       conv_out=ncs[l, s], gv_out=ngs[l, s])
            conv_taps(64)
            conv_post(64, 0, cat[0:64, s, :], "cat%d" % s)
            sample_attention(l, s)
        for s in range(NB):
            tile_back(l, 64, x_res[0:64, s, :], "x%d" % s, s, s * 64, cat[0:64, s, :], "cat%d" % s)
        S.barrier()
        mlp_phase(l, 64, [[(x_res[0:64, s, :], "x%d" % s) for s in range(NB)]])
        S.barrier()
    for s in range(NB if do_sample else 0):
        sp_add(i_dma(ys[s], x_res[0:64, s, :]), r=["x%d" % s])

    S.finalize_and_emit()
    return nc


_NC_CACHE = {}


def _consts():
    ident = np.eye(128, dtype=np.float32)
    tril = np.tril(np.ones((128, 128), dtype=np.float32))
    freqs = (500000.0 ** (-np.arange(0, 16, 2, dtype=np.float32) / 16.0)).astype(np.float32)
    pos_p = np.arange(SEQ, dtype=np.float32)
    ang_p = (pos_p[:, None] * freqs[None, :]).astype(np.float32)
    rp = np.concatenate([np.cos(ang_p), np.sin(ang_p)], axis=1).astype(np.float32)
    rp = np.ascontiguousarray(rp.reshape(16, 128, 16).transpose(1, 0, 2))
    pos_s = (PAST + np.arange(DSEQ)).astype(np.float32)
    ang_s = (pos_s[:, None] * freqs[None, :]).astype(np.float32)
    rs = np.concatenate([np.cos(ang_s), np.sin(ang_s)], axis=1).astype(np.float32)
    return ident, tril, rp, np.ascontiguousarray(rs)


def kernel(x_prompt, x_sample, cache_k, cache_v, cache_conv, norm_mix_pre, norm_mix_post,
           norm_mlp_pre, norm_mlp_post, w_in, diff_lambda, diff_subln, gmlp_norm, gmlp_w_s,
           gmlp_bias, conv_w, conv_b, conv_ln_gain, conv_ln_bias, w_out, w_up, w_down):
    n = 8
    f = lambda a: np.ascontiguousarray(np.asarray(a, dtype=np.float32))
    if "nc" not in _NC_CACHE:
        _NC_CACHE["nc"] = build_nc()
    nc = _NC_CACHE["nc"]
    ident, tril, rp, rs = _consts()
    x_prompt = f(x_prompt)
    x_sample = f(x_sample)
    ckf = f(cache_k).reshape(DEPTH, 32, PAST, 512)
    cvf = f(cache_v).reshape(DEPTH, 32, PAST, 512)
    ccf = np.ascontiguousarray(f(cache_conv).reshape(DEPTH, 32, CW - 1, 2, 128).transpose(0, 1, 4, 3, 2))
    gpre = np.ascontiguousarray(np.stack([f(norm_mix_pre).reshape(DEPTH, 8, 128),
                                          f(norm_mlp_pre).reshape(DEPTH, 8, 128)], axis=1).transpose(0, 3, 1, 2))
    convw_t = np.ascontiguousarray(f(conv_w).reshape(DEPTH, CW, 2, 128).transpose(0, 3, 2, 1))
    convb_t = np.ascontiguousarray(f(conv_b).reshape(DEPTH, 2, 128).transpose(0, 2, 1))
    gbias_t = np.ascontiguousarray(f(gmlp_bias).transpose(0, 2, 1))
    shared = {
        "norm_pre_t": gpre, "norm_mix_post": f(norm_mix_post),
        "norm_mlp_post": f(norm_mlp_post),
        "w_in": f(w_in), "diff_lambda": f(diff_lambda).reshape(DEPTH, 256), "diff_subln": f(diff_subln),
        "gmlp_norm": f(gmlp_norm), "gmlp_w_s": f(gmlp_w_s), "gmlp_bias_t": gbias_t,
        "conv_w_t": convw_t, "conv_b_t": convb_t, "conv_ln_gain": f(conv_ln_gain),
        "conv_ln_bias": f(conv_ln_bias), "w_out": f(w_out), "w_up": f(w_up), "w_down": f(w_down),
        "c_ident": ident, "c_tril": tril, "c_rope_p": rp, "c_rope_s": rs,
    }
    in_maps = []
    for c in range(n):
        sl = slice(c * NB, (c + 1) * NB)
        m = dict(shared)
        m["x_prompt"] = np.ascontiguousarray(x_prompt[sl])
        m["x_sample"] = np.ascontiguousarray(x_sample[sl])
        m["cache_k"] = np.ascontiguousarray(ckf[:, sl])
        m["cache_v"] = np.ascontiguousarray(cvf[:, sl])
        m["cache_conv_t"] = np.ascontiguousarray(ccf[:, sl])
        in_maps.append(m)
    res = run_bass_kernel_spmd(nc, in_maps, core_ids=list(range(n)))
    R = res.results
    cat0 = lambda k: np.concatenate([r[k] for r in R], axis=0)
    cat1 = lambda k: np.concatenate([r[k] for r in R], axis=1)
    y_prompt = cat0("y_prompt")
    y_sample = cat0("y_sample")
    nkp = cat1("new_k_prompt").reshape(DEPTH, 32, SEQ, 8, 64)
    nvp = cat1("new_v_prompt").reshape(DEPTH, 32, SEQ, 4, 128)
    ncp = cat1("new_conv_prompt")
    nks = cat1("new_k_sample").reshape(DEPTH, 32, DSEQ, 8, 64)
    nvs = cat1("new_v_sample").reshape(DEPTH, 32, DSEQ, 4, 128)
    ncs = cat1("new_conv_sample")
    ngs = cat1("new_gmlp_v_sample").reshape(DEPTH, 32, DSEQ, 4, 64)
    return (y_prompt, y_sample, nkp, nvp, ncp, nks, nvs, ncs, ngs)
```
